# Optimizing a Trainium2 kernel written in Bass

```python
import jax, jax.numpy as jnp
from jax import lax
import numpy as np

D_MODEL = 1024
BATCH = 8
SEQ = 2048
DEPTH = 2

N_HEADS = 8
D_NOPE = 64
D_ROPE = 32
D_QK = D_NOPE + D_ROPE
D_V = 64
Q_LORA = 256
KV_LORA = 128
ATTN_WIDTH = N_HEADS * D_V
ROPE_THETA = 10000.0
Q_BLOCK = 128
CONV_WIDTH = 512
CONV_K = 3
D_FF = 2816
FFN_CONV_K = 3
PLE_DIM = 256
EPS = 1e-6
SPLITS = (
    Q_LORA,
    Q_LORA + KV_LORA,
    Q_LORA + KV_LORA + D_ROPE,
    Q_LORA + KV_LORA + D_ROPE + CONV_WIDTH,
    Q_LORA + KV_LORA + D_ROPE + 2 * CONV_WIDTH,
    Q_LORA + KV_LORA + D_ROPE + 3 * CONV_WIDTH,
)
IN_COLS = SPLITS[-1] + 2 * D_MODEL

kernel_name = "hybrid_mla_shortconv_convffn_ple_encoder"


def rmsnorm(x, g):
    xf = x.astype(jnp.float32)
    y = xf * lax.rsqrt(jnp.mean(xf * xf, axis=-1, keepdims=True) + EPS)
    return (y * g.astype(jnp.float32)).astype(x.dtype)


def dwconv(u, w):
    c = u.shape[-1]
    k = w.shape[0]
    pad = (k - 1) // 2
    return lax.conv_general_dilated(
        u, w[:, None, :].astype(u.dtype), window_strides=(1,), padding=[(pad, pad)],
        dimension_numbers=("NWC", "WIO", "NWC"), feature_group_count=c)


def rope_tables(positions):
    inv = ROPE_THETA ** (-jnp.arange(0, D_ROPE, 2, dtype=jnp.float32) / D_ROPE)
    ang = positions.astype(jnp.float32)[..., None] * inv
    return jnp.cos(ang)[:, :, None, :], jnp.sin(ang)[:, :, None, :]


def apply_rope(t, cos, sin):
    tf = t.astype(jnp.float32)
    t1, t2 = jnp.split(tf, 2, axis=-1)
    out = jnp.concatenate([t1 * cos - t2 * sin, t2 * cos + t1 * sin], axis=-1)
    return out.astype(t.dtype)


def split_norm(t, g):
    return jnp.concatenate([rmsnorm(t[..., :D_NOPE], g[:D_NOPE]),
                            rmsnorm(t[..., D_NOPE:], g[D_NOPE:])], axis=-1)


def block_attention(q, k, v):
    b, s, h, _ = q.shape
    nq = s // Q_BLOCK
    qb = q.reshape(b, nq, Q_BLOCK, h, D_QK).transpose(1, 0, 2, 3, 4)
    scale = D_QK ** -0.5

    def one_block(qi):
        sc = jnp.einsum("bqhd,bkhd->bhqk", qi, k).astype(jnp.float32) * scale
        pr = jax.nn.softmax(sc, axis=-1).astype(v.dtype)
        return jnp.einsum("bhqk,bkhd->bqhd", pr, v)

    o = lax.map(one_block, qb)
    return o.transpose(1, 0, 2, 3, 4).reshape(b, s, h * D_V)


def setup_inputs(seed: int = 0) -> dict:
    key = jax.random.key(seed)
    ks = jax.random.split(key, 24)
    f32 = jnp.float32

    def nrm(k, shape, fan_in, scale=1.0):
        return jax.random.normal(k, shape, f32) * (scale * fan_in ** -0.5)

    def gain(k, shape):
        return 1.0 + 0.02 * jax.random.normal(k, shape, f32)

    offsets = jax.random.randint(ks[2], (BATCH, 1), 0, 4096, dtype=jnp.int32)
    positions = (offsets + jnp.arange(SEQ, dtype=jnp.int32)[None, :]).astype(jnp.int32)
    return {
        "x": jax.random.normal(ks[0], (BATCH, SEQ, D_MODEL), f32),
        "p": jax.random.normal(ks[1], (DEPTH, BATCH, SEQ, PLE_DIM), f32),
        "positions": positions,
        "w_in": nrm(ks[3], (DEPTH, D_MODEL, IN_COLS), D_MODEL),
        "b_gate": 0.01 * jax.random.normal(ks[4], (DEPTH, 2 * D_MODEL), f32),
        "g_mix": gain(ks[5], (DEPTH, D_MODEL)),
        "g_q_lat": gain(ks[6], (DEPTH, Q_LORA)),
        "w_uq": nrm(ks[7], (DEPTH, Q_LORA, N_HEADS * D_QK), Q_LORA),
        "g_kv_lat": gain(ks[8], (DEPTH, KV_LORA)),
        "w_ukv": nrm(ks[9], (DEPTH, KV_LORA, N_HEADS * (D_NOPE + D_V)), KV_LORA),
        "g_q_head": gain(ks[10], (DEPTH, D_QK)),
        "g_k_head": gain(ks[11], (DEPTH, D_QK)),
        "w_attn_up": nrm(ks[12], (DEPTH, ATTN_WIDTH, D_MODEL), ATTN_WIDTH),
        "w_conv": nrm(ks[13], (DEPTH, CONV_K, CONV_WIDTH), CONV_K),
        "w_conv_up": nrm(ks[14], (DEPTH, CONV_WIDTH, D_MODEL), CONV_WIDTH),
        "w_o": nrm(ks[15], (DEPTH, D_MODEL, D_MODEL), D_MODEL, 0.5),
        "g_ffn": gain(ks[16], (DEPTH, D_MODEL)),
        "w_up": nrm(ks[17], (DEPTH, D_MODEL, 2 * D_FF), D_MODEL),
        "w_ffn_conv": nrm(ks[18], (DEPTH, FFN_CONV_K, 2 * D_FF), FFN_CONV_K),
        "w_down": nrm(ks[19], (DEPTH, D_FF, D_MODEL), D_FF, 0.5),
        "g_ple": gain(ks[20], (DEPTH, D_MODEL)),
        "w_ple_gate": nrm(ks[21], (DEPTH, D_MODEL, D_MODEL), D_MODEL),
        "w_ple": nrm(ks[22], (DEPTH, PLE_DIM, D_MODEL), PLE_DIM, 0.5),
    }


def reference(x, p, positions, w_in, b_gate, g_mix, g_q_lat, w_uq, g_kv_lat, w_ukv,
              g_q_head, g_k_head, w_attn_up, w_conv, w_conv_up, w_o, g_ffn, w_up,
              w_ffn_conv, w_down, g_ple, w_ple_gate, w_ple):
    b, s, _ = x.shape
    cos, sin = rope_tables(positions)
    for i in range(DEPTH):
        h = rmsnorm(x, g_mix[i])
        z = h @ w_in[i]
        c_q, c_kv, k_r, cb, cc, cx, gates = jnp.split(z, SPLITS, axis=-1)

        q = (rmsnorm(c_q, g_q_lat[i]) @ w_uq[i]).reshape(b, s, N_HEADS, D_QK)
        kv = (rmsnorm(c_kv, g_kv_lat[i]) @ w_ukv[i]).reshape(b, s, N_HEADS, D_NOPE + D_V)
        k_nope, v = kv[..., :D_NOPE], kv[..., D_NOPE:]
        k_rope = jnp.broadcast_to(k_r[:, :, None, :], (b, s, N_HEADS, D_ROPE))
        k = jnp.concatenate([k_nope, k_rope], axis=-1)
        q = split_norm(q, g_q_head[i])
        k = split_norm(k, g_k_head[i])
        q = jnp.concatenate([q[..., :D_NOPE], apply_rope(q[..., D_NOPE:], cos, sin)], axis=-1)
        k = jnp.concatenate([k[..., :D_NOPE], apply_rope(k[..., D_NOPE:], cos, sin)], axis=-1)
        y_attn = block_attention(q, k, v)

        y_conv = cb * dwconv(cc * cx, w_conv[i])

        g = jax.nn.sigmoid((gates + b_gate[i]).astype(jnp.float32)).astype(x.dtype)
        g_a, g_c = g[..., :D_MODEL], g[..., D_MODEL:]
        merged = g_a * (y_attn @ w_attn_up[i]) + g_c * (y_conv @ w_conv_up[i])
        x = x + merged @ w_o[i]

        h = rmsnorm(x, g_ffn[i])
        u = dwconv(h @ w_up[i], w_ffn_conv[i])
        a, val = u[..., :D_FF], u[..., D_FF:]
        x = x + (jax.nn.silu(a) * val) @ w_down[i]

        h = rmsnorm(x, g_ple[i])
        pg = jax.nn.sigmoid((h @ w_ple_gate[i]).astype(jnp.float32)).astype(x.dtype)
        x = x + pg * (p[i] @ w_ple[i])
    return x
```

```python
import numpy as np
from contextlib import ExitStack
import concourse.bass as bass
import concourse.mybir as mybir
from concourse.bass_utils import run_bass_kernel_spmd

F32 = mybir.dt.float32
BF16 = mybir.dt.bfloat16
I32 = mybir.dt.int32
AF = mybir.ActivationFunctionType
ALU = mybir.AluOpType

DEPTH = 2
D = 1024
T = 2048
NH = 8
DQK = 96
QL = 256
KVL = 128
DFF = 2816
INC = 4000
EPS = 1e-6
NCORES = 8
SCALE = float(DQK ** -0.5)
MAGIC = 12582912.0

V_GMIX, V_GFFN, V_GPLE = 0, 8, 16
V_GQL, V_GKVL = 24, 26
V_BG = 27
V_WC = 43
V_WFC = 55
V_GQ, V_GQSW, V_GK, V_GKSW = 187, 188, 189, 190
V_INV, V_SGN = 191, 192
NV = 196


class Ev:
    __slots__ = ("sem", "val")

    def __init__(self, sem, val):
        self.sem = sem
        self.val = val


class Reg:
    __slots__ = ("name", "w", "r")

    def __init__(self, name=""):
        self.name = name
        self.w = None
        self.r = []


class _Rec:
    def __init__(self):
        self.calls = []

    def __getattr__(self, name):
        def f(*a, **k):
            self.calls.append((name, a, k))
            return None
        return f


class Eng:
    def __init__(self, name, sem, self_sync=True):
        self.name = name
        self.sem = sem
        self.seq = 0
        self.seen = {}
        self.thunks = []
        self.self_sync = self_sync

    def wait(self, ev):
        if ev is None:
            return
        if (not self.self_sync) and ev.sem is self.sem:
            return
        k = id(ev.sem)
        if self.seen.get(k, 0) >= ev.val:
            return
        self.seen[k] = ev.val
        sem, val = ev.sem, ev.val
        self.thunks.append(lambda e: e.wait_ge(sem, val))

    def deps(self, reads, writes):
        for r in reads:
            self.wait(r.w)
        for w in writes:
            self.wait(w.w)
            for e in w.r:
                self.wait(e)

    def _mark(self, ev, reads, writes):
        for r in reads:
            r.r.append(ev)
            if len(r.r) > 64:
                r.r = r.r[-48:]
        for w in writes:
            w.w = ev
            w.r = []

    def op(self, fn, reads=(), writes=()):
        self.deps(reads, writes)
        self.seq += 1
        sem = self.sem
        rec = _Rec()
        fn(rec)
        calls = rec.calls
        assert calls

        def thunk(e, calls=calls, sem=sem):
            inst = None
            for name, a, k in calls:
                inst = getattr(e, name)(*a, **k)
            inst.then_inc(sem, 1)
        self.thunks.append(thunk)
        ev = Ev(sem, self.seq)
        self._mark(ev, reads, writes)
        return ev

    def dma(self, out, in_, ds, reads=(), writes=(), mark=True):
        self.deps(reads, writes)
        ds[1] += 16
        sem = ds[0]
        self.thunks.append(lambda e: e.dma_start(out=out, in_=in_).then_inc(sem, 16))
        ev = Ev(sem, ds[1])
        if mark:
            self._mark(ev, reads, writes)
        return ev


def build_program(nc, layer_ids, dbg=False):
    L = DEPTH
    dram_in = lambda name, shape, dt=F32: nc.dram_tensor(name, list(shape), dt, kind="ExternalInput").ap()
    xT_d = dram_in("xT", [D, T])
    pT_d = dram_in("pT", [L, 256, T])
    pos_d = dram_in("pos", [1, T], I32)
    vecs_d = dram_in("vecs", [128, L * NV])
    w_in_d = dram_in("w_in", [L, D, INC])
    w_kr_d = dram_in("w_kr", [L, D, 96])
    w_krsw_d = dram_in("w_kr_sw", [L, D, 96])
    w_uq_d = dram_in("w_uq", [L, QL, NH * DQK])
    w_uqsw_d = dram_in("w_uq_sw", [L, QL, NH * DQK])
    w_ukv_d = dram_in("w_ukv", [L, KVL, NH * 128])
    w_au_d = dram_in("w_attn_up", [L, 512, D])
    w_cu_d = dram_in("w_conv_up", [L, 512, D])
    w_o_d = dram_in("w_o", [L, D, D])
    w_up_d = dram_in("w_up", [L, D, 2 * DFF])
    w_dn_d = dram_in("w_down", [L, DFF, D])
    w_pg_d = dram_in("w_ple_gate", [L, D, D])
    w_pl_d = dram_in("w_ple", [L, 256, D])
    outT_d = nc.dram_tensor("outT", [D, T], F32, kind="ExternalOutput").ap()
    dbg_out = {}

    with ExitStack() as st:
        _cnt = [0]

        def sb(name, shape, dt, stack=st):
            _cnt[0] += 1
            return stack.enter_context(nc.sbuf_tensor(f"sb{_cnt[0]}_{name}", list(shape), dt))

        def new_sem(name):
            return st.enter_context(nc.semaphore(name))

        pe = Eng("pe", new_sem("s_pe"), self_sync=False)
        act = Eng("act", new_sem("s_act"))
        dve = Eng("dve", new_sem("s_dve"))
        pool = Eng("pool", new_sem("s_pool"))
        sp = Eng("sp", new_sem("s_sp"))
        engines = [pe, act, dve, pool, sp]
        all_ds = []

        def new_ds(name):
            ds = [new_sem(name), 0]
            all_ds.append(ds)
            return ds

        def barrier():
            for e in engines:
                for x in engines:
                    if x is not e and x.seq > 0:
                        e.wait(Ev(x.sem, x.seq))
                for ds in all_ds:
                    if ds[1] > 0:
                        e.wait(Ev(ds[0], ds[1]))

        xT = sb("xT", [128, 8, T], F32)
        r_x = [Reg(f"x{c}") for c in range(8)]
        rstd = sb("rstd", [128, T], F32)
        r_rstd = Reg("rstd")
        vecs = sb("vecs", [128, L * NV], F32)
        r_vecs = Reg("vecs")
        ones_b = sb("ones_b", [128, 128], BF16)
        blk_b = sb("blk_b", [128, 96], BF16)
        ones_f = sb("ones_f", [128, 64], F32)
        epsc = sb("epsc", [128, 1], F32)
        r_const = Reg("const")
        NS = 3
        SLOT = 4096
        wring = [sb(f"wring{i}", [128, SLOT], BF16) for i in range(NS)]
        r_ring = [Reg(f"ring{i}") for i in range(NS)]
        ds_ring = [new_ds(f"d_ring{i}") for i in range(NS)]
        WSM = 5632
        wsmall = sb("wsmall", [128, WSM], BF16)
        r_wsm = Reg("wsmall")
        ds_wsm = new_ds("d_wsm")
        ps = st.enter_context(nc.psum_tensor("ps", [128, 8 * 512], F32))
        r_ps = [Reg(f"ps{i}") for i in range(8)]

        def psb(b0, nb=1, m=128, p0=0):
            return ps[p0:p0 + m, b0 * 512:(b0 + nb) * 512]

        def rps(b0, nb=1):
            return r_ps[b0:b0 + nb]

        ds_x = [new_ds(f"d_x{c}") for c in range(8)]
        ds_misc = new_ds("d_misc")
        ds_pos = new_ds("d_pos")
        ds_p = new_ds("d_p")
        ds_yodd = new_ds("d_yodd")
        ds_out = new_ds("d_out")
        ds_dbg = new_ds("d_dbg")

        def vcol(l, col, p0=0, p1=128, n=1):
            return vecs[p0:p1, l * NV + col:l * NV + col + n]

        def dump(name, ap, reg_list, shape):
            if not dbg:
                return
            t = nc.dram_tensor("dbg_" + name, list(shape), ap.dtype, kind="ExternalOutput").ap()
            dbg_out[name] = t
            sp.dma(t, ap, ds_dbg, reads=reg_list)

        requests = []
        req_index = {}
        state = {"issued": 0, "released": set()}

        def add_req(key, ap, kc, n):
            assert kc * n <= SLOT, (key, kc, n)
            req_index[key] = len(requests)
            requests.append((key, ap, kc, n))

        def panel_rows(w2d, r0, kc, c0, n):
            return w2d[r0:r0 + kc * 128, c0:c0 + n].rearrange("(kc p) n -> p kc n", p=128)

        for l in layer_ids:
            wi = w_in_d[l]
            add_req((l, "lat"), panel_rows(wi, 0, 8, 0, 416), 8, 416)
            add_req((l, "cc"), panel_rows(wi, 0, 8, 928, 512), 8, 512)
            add_req((l, "cx"), panel_rows(wi, 0, 8, 1440, 512), 8, 512)
            add_req((l, "cb"), panel_rows(wi, 0, 8, 416, 512), 8, 512)
            for hh in range(2):
                for jg in range(2):
                    add_req((l, "ga", hh, jg), panel_rows(wi, 0, 8, 1952 + jg * 512, 512), 8, 512)
                    add_req((l, "gc", hh, jg), panel_rows(wi, 0, 8, 2976 + jg * 512, 512), 8, 512)
                    if jg == 0:
                        add_req((l, "wau", hh), panel_rows(w_au_d[l], 0, 4, 0, 1024), 4, 1024)
                for mg in range(2):
                    add_req((l, "wo", hh, mg), panel_rows(w_o_d[l], 0, 8, mg * 512, 512), 8, 512)
            for g, (p_lo, p_hi) in enumerate(FFN_GROUPS):
                for pp in range(p_lo, p_hi):
                    n = 512 if pp < 5 else 256
                    add_req((l, "ua", pp), panel_rows(w_up_d[l], 0, 8, pp * 512, n), 8, n)
                    add_req((l, "uv", pp), panel_rows(w_up_d[l], 0, 8, DFF + pp * 512, n), 8, n)
                for pp in range(p_lo, p_hi):
                    kc = 4 if pp < 5 else 2
                    add_req((l, "dn", pp), panel_rows(w_dn_d[l], pp * 512, kc, 0, 1024), kc, 1024)
            add_req((l, "pl"), panel_rows(w_pl_d[l], 0, 2, 0, 1024), 2, 1024)
            for mg in range(2):
                add_req((l, "pg", mg), panel_rows(w_pg_d[l], 0, 8, mg * 512, 512), 8, 512)

        def pump(upto):
            while state["issued"] < len(requests) and state["issued"] <= upto:
                i = state["issued"]
                if i >= NS and (i - NS) not in state["released"]:
                    break
                key, ap, kc, n = requests[i]
                s = i % NS
                view = wring[s][:, 0:kc * n].rearrange("p (a b) -> p a b", b=n)
                pool.dma(view, ap, ds_ring[s], writes=[r_ring[s]])
                state["issued"] += 1

        def wget(key):
            i = req_index[key]
            pump(i + NS - 1)
            assert state["issued"] > i, ("ring deadlock", key)
            key, ap, kc, n = requests[i]
            s = i % NS
            view = wring[s][:, 0:kc * n].rearrange("p (a b) -> p a b", b=n)
            return view, r_ring[s]

        def wrel(key):
            i = req_index[key]
            state["released"].add(i)
            pump(i + NS)

        def mm_group(out_fn, M, lhs_list, rhs_fn, ntb, reads, writes, p0=0):
            nk = len(lhs_list)

            def fn(e):
                for tb in range(ntb):
                    for k in range(nk):
                        e.matmul(out_fn(tb), lhsT=lhs_list[k], rhs=rhs_fn(k, tb),
                                 start=(k == 0), stop=(k == nk - 1), skip_group_check=True)
            return pe.op(fn, reads=reads, writes=writes)

        def norm_stats(sq_bufs, r_sq, ones_lhs):
            for c in range(8):
                b = c % 2
                act.op(lambda e, c=c, b=b: e.activation(out=sq_bufs[b][:, :], in_=xT[:, c, :], func=AF.Square),
                       reads=[r_x[c]], writes=[r_sq[b]])

                def fn(e, c=c, b=b):
                    last = None
                    for tb in range(4):
                        last = e.matmul(psb(tb), lhsT=ones_lhs, rhs=sq_bufs[b][:, tb * 512:(tb + 1) * 512],
                                        start=(c == 0), stop=(c == 7), skip_group_check=True)
                    return last
                pe.op(fn, reads=[r_sq[b], r_const], writes=rps(0, 4))
            act.op(lambda e: e.activation(out=rstd[:, :], in_=psb(0, 4), func=AF.Sqrt, bias=epsc[:, 0:1], scale=1.0 / D),
                   reads=rps(0, 4) + [r_const], writes=[r_rstd])
            dve.op(lambda e: e.reciprocal(out=rstd[:, :], in_=rstd[:, :]), reads=[r_rstd], writes=[r_rstd])

        def make_hT(hT, r_h, l, gcol):
            for c in range(8):
                eng = dve
                eng.op(lambda e, c=c: e.scalar_tensor_tensor(out=hT[:, c, :], in0=xT[:, c, :],
                                                             scalar=vcol(l, gcol + c), in1=rstd[:, :],
                                                             op0=ALU.mult, op1=ALU.mult),
                       reads=[r_x[c], r_rstd, r_vecs], writes=[r_h[c]])

        def rms_from_psum(dst, r_dst, src_banks, nb, m, scale):
            act.op(lambda e: e.activation(out=dst, in_=psb(src_banks, nb, m), func=AF.Sqrt, bias=epsc[0:m, 0:1], scale=scale),
                   reads=rps(src_banks, nb) + [r_const], writes=[r_dst])
            dve.op(lambda e: e.reciprocal(out=dst, in_=dst), reads=[r_dst], writes=[r_dst])

        for c in range(8):
            sp.dma(xT[:, c, :], xT_d[c * 128:(c + 1) * 128, :], ds_x[c], writes=[r_x[c]])
        sp.dma(vecs[:, :], vecs_d[:, :], ds_misc, writes=[r_vecs])
        pool.op(lambda e: e.memset(ones_b[:, :], 1.0), writes=[r_const])
        pool.op(lambda e: e.memset(blk_b[:, :], 0.0), writes=[r_const])
        pool.op(lambda e: e.memset(blk_b[0:64, 0:64], 1.0 / 64), writes=[r_const])
        pool.op(lambda e: e.memset(blk_b[64:96, 64:96], 1.0 / 32), writes=[r_const])
        pool.op(lambda e: e.memset(ones_f[:, :], 1.0), writes=[r_const])
        pool.op(lambda e: e.memset(epsc[:, :], EPS), writes=[r_const])

        for l in layer_ids:
            wuq = wsmall[:, 0:1536].rearrange("p (a b) -> p a b", b=768)
            wuqsw = wsmall[:, 1536:3072].rearrange("p (a b) -> p a b", b=768)
            wukv = wsmall[:, 3072:4096]
            wkr = wsmall[:, 4096:4864].rearrange("p (a b) -> p a b", b=96)
            wkrsw = wsmall[:, 4864:5632].rearrange("p (a b) -> p a b", b=96)
            pool.dma(wuq, w_uq_d[l].rearrange("(kc p) n -> p kc n", p=128), ds_wsm, writes=[r_wsm], mark=False)
            pool.dma(wuqsw, w_uqsw_d[l].rearrange("(kc p) n -> p kc n", p=128), ds_wsm, mark=False)
            pool.dma(wukv, w_ukv_d[l], ds_wsm, mark=False)
            pool.dma(wkr, w_kr_d[l].rearrange("(kc p) n -> p kc n", p=128), ds_wsm, mark=False)
            ev = pool.dma(wkrsw, w_krsw_d[l].rearrange("(kc p) n -> p kc n", p=128), ds_wsm, mark=False)
            r_wsm.w = ev
            r_wsm.r = []

            with ExitStack() as sL:
                yattn = sb("yattn", [128, 4, T], BF16, sL)
                r_ya = [Reg(f"ya{c}") for c in range(4)]
                with ExitStack() as sA:
                    cqn = sb("cqn", [128, 2, T], BF16, sA)
                    r_cqn = Reg("cqn")
                    ckvn = sb("ckvn", [128, T], BF16, sA)
                    r_ckvn = Reg("ckvn")
                    kropeT = sb("kropeT", [128, T], BF16, sA)
                    r_krope = Reg("krope")
                    cosT = sb("cosT", [128, T], F32, sA)
                    sinT = sb("sinT", [128, T], F32, sA)
                    r_tab = Reg("tab")
                    with ExitStack() as s0:
                        pos_i = sb("pos_i", [128, T], I32, s0)
                        r_pos = Reg("pos")
                        ang = sb("ang", [128, T], F32, s0)
                        r_ang = Reg("ang")
                        sp.dma(pos_i[64:96, :], pos_d[0:1, :].partition_broadcast(32), ds_pos, writes=[r_pos])
                        R = slice(64, 96)
                        dve.op(lambda e: e.tensor_copy(out=ang[R, :], in_=pos_i[R, :]), reads=[r_pos], writes=[r_ang])
                        dve.op(lambda e: e.tensor_scalar(out=ang[R, :], in0=ang[R, :], scalar1=vcol(l, V_INV, 64, 96), scalar2=None,
                                                         op0=ALU.mult), reads=[r_ang, r_vecs], writes=[r_ang])
                        for which, dst, off in (("sin", sinT, 0.0), ("cos", cosT, float(np.pi / 2))):
                            dve.op(lambda e, dst=dst, off=off: e.tensor_scalar(
                                out=dst[R, :], in0=ang[R, :], scalar1=off, scalar2=float(1.0 / (2 * np.pi)),
                                op0=ALU.add, op1=ALU.mult), reads=[r_ang], writes=[r_tab])
                            dve.op(lambda e, dst=dst: e.tensor_scalar(out=dst[R, :], in0=dst[R, :], scalar1=MAGIC, scalar2=None,
                                                                      op0=ALU.add), reads=[r_tab], writes=[r_tab])
                            dve.op(lambda e, dst=dst: e.tensor_scalar(out=dst[R, :], in0=dst[R, :], scalar1=MAGIC, scalar2=None,
                                                                      op0=ALU.subtract), reads=[r_tab], writes=[r_tab])
                            dve.op(lambda e, dst=dst: e.scalar_tensor_tensor(
                                out=dst[R, :], in0=dst[R, :], scalar=float(-2 * np.pi), in1=ang[R, :],
                                op0=ALU.mult, op1=ALU.add), reads=[r_tab, r_ang], writes=[r_tab])
                            dve.op(lambda e, dst=dst, off=off: e.tensor_scalar(
                                out=dst[R, :], in0=dst[R, :], scalar1=off, scalar2=float(np.pi),
                                op0=ALU.add, op1=ALU.min), reads=[r_tab], writes=[r_tab])
                            dve.op(lambda e, dst=dst: e.tensor_scalar(out=dst[R, :], in0=dst[R, :], scalar1=float(-np.pi), scalar2=None,
                                                                      op0=ALU.max), reads=[r_tab], writes=[r_tab])
                            act.op(lambda e, dst=dst: e.activation(out=dst[R, :], in_=dst[R, :], func=AF.Sin),
                                   reads=[r_tab], writes=[r_tab])
                        dve.op(lambda e: e.tensor_scalar(out=sinT[R, :], in0=sinT[R, :], scalar1=vcol(l, V_SGN, 64, 96), scalar2=None,
                                                         op0=ALU.mult), reads=[r_tab, r_vecs], writes=[r_tab])
                        pool.op(lambda e: e.memset(cosT[0:64, :], 1.0), writes=[r_tab])
                        pool.op(lambda e: e.memset(sinT[0:64, :], 0.0), writes=[r_tab])
                        if l == layer_ids[0]:
                            dump("cosT", cosT[0:96, :], [r_tab], [96, T])
                            dump("sinT", sinT[0:96, :], [r_tab], [96, T])
                    barrier()
                    with ExitStack() as s1:
                        hT = sb("hT", [128, 8, T], BF16, s1)
                        r_h = [Reg(f"h{c}") for c in range(8)]
                        sq0 = sb("sq0", [128, T], BF16, s1)
                        r_sq = [Reg("sq0"), Reg("sq0")]
                        r_sq[1] = r_sq[0]
                        tmpa = sb("tmpa", [128, 1024], F32, s1)
                        tmpb = sb("tmpb", [128, 1024], F32, s1)
                        tmpc = sb("tmpc", [128, 1024], F32, s1)
                        r_ta, r_tb, r_tc = Reg("ta"), Reg("tb"), Reg("tc")
                        norm_stats([sq0, sq0], r_sq, ones_b[:, :])
                        make_hT(hT, r_h, l, V_GMIX)
                        if l == layer_ids[0]:
                            dump("hT", hT[:, :, :], r_h, [128, 8, T])

                        wlat, r_wlat = wget((l, "lat"))
                        sqh = sq0[:, 0:2048].rearrange("p (a b) -> p a b", b=1024)
                        for hh in range(2):
                            t0 = hh * 1024
                            for c in range(2):
                                mm_group(lambda tb, c=c: psb(2 * c + tb), 128,
                                         [wlat[:, k, c * 128:(c + 1) * 128] for k in range(8)],
                                         lambda k, tb: hT[:, k, t0 + tb * 512:t0 + (tb + 1) * 512], 2,
                                         reads=[r_wlat] + r_h, writes=rps(2 * c, 2))
                                act.op(lambda e, c=c: e.activation(out=sqh[:, c, :], in_=psb(2 * c, 2), func=AF.Square),
                                       reads=rps(2 * c, 2), writes=[r_sq[0]])
                            mm_group(lambda tb: psb(4 + tb), 128, [ones_b[:, :], ones_b[:, :]],
                                     lambda k, tb: sqh[:, k, tb * 512:(tb + 1) * 512], 2,
                                     reads=[r_sq[0], r_const], writes=rps(4, 2))
                            rms_from_psum(tmpa[:, :], r_ta, 4, 2, 128, 1.0 / QL)
                            for c in range(2):
                                dve.op(lambda e, c=c: e.scalar_tensor_tensor(
                                    out=cqn[:, c, t0:t0 + 1024], in0=psb(2 * c, 2), scalar=vcol(l, V_GQL + c), in1=tmpa[:, :],
                                    op0=ALU.mult, op1=ALU.mult), reads=rps(2 * c, 2) + [r_ta, r_vecs], writes=[r_cqn])
                            mm_group(lambda tb: psb(tb), 128, [wlat[:, k, 256:384] for k in range(8)],
                                     lambda k, tb: hT[:, k, t0 + tb * 512:t0 + (tb + 1) * 512], 2,
                                     reads=[r_wlat] + r_h, writes=rps(0, 2))
                            act.op(lambda e: e.activation(out=sqh[:, 0, :], in_=psb(0, 2), func=AF.Square),
                                   reads=rps(0, 2), writes=[r_sq[0]])
                            mm_group(lambda tb: psb(4 + tb), 128, [ones_b[:, :]],
                                     lambda k, tb: sqh[:, 0, tb * 512:(tb + 1) * 512], 2,
                                     reads=[r_sq[0], r_const], writes=rps(4, 2))
                            rms_from_psum(tmpa[:, :], r_ta, 4, 2, 128, 1.0 / KVL)
                            dve.op(lambda e: e.scalar_tensor_tensor(
                                out=ckvn[:, t0:t0 + 1024], in0=psb(0, 2), scalar=vcol(l, V_GKVL), in1=tmpa[:, :],
                                op0=ALU.mult, op1=ALU.mult), reads=rps(0, 2) + [r_ta, r_vecs], writes=[r_ckvn])
                            mm_group(lambda tb: psb(tb, 1, 96), 96, [wkr[:, k, :] for k in range(8)],
                                     lambda k, tb: hT[:, k, t0 + tb * 512:t0 + (tb + 1) * 512], 2,
                                     reads=[r_wsm] + r_h, writes=rps(0, 2))
                            mm_group(lambda tb: psb(2 + tb, 1, 96), 96, [wkrsw[:, k, :] for k in range(8)],
                                     lambda k, tb: hT[:, k, t0 + tb * 512:t0 + (tb + 1) * 512], 2,
                                     reads=[r_wsm] + r_h, writes=rps(2, 2))
                            act.op(lambda e: e.activation(out=sqh[0:96, 0, :], in_=psb(0, 2, 96), func=AF.Square),
                                   reads=rps(0, 2), writes=[r_sq[0]])
                            mm_group(lambda tb: psb(4 + tb, 1, 96), 96, [blk_b[0:96, 0:96]],
                                     lambda k, tb: sqh[0:96, 0, tb * 512:(tb + 1) * 512], 2,
                                     reads=[r_sq[0], r_const], writes=rps(4, 2))
                            rms_from_psum(tmpa[0:96, :], r_ta, 4, 2, 96, 1.0)
                            dve.op(lambda e: e.scalar_tensor_tensor(
                                out=tmpb[0:96, :], in0=psb(0, 2, 96), scalar=vcol(l, V_GK, 0, 96), in1=cosT[0:96, t0:t0 + 1024],
                                op0=ALU.mult, op1=ALU.mult), reads=rps(0, 2) + [r_tab, r_vecs], writes=[r_tb])
                            dve.op(lambda e: e.scalar_tensor_tensor(
                                out=tmpc[0:96, :], in0=psb(2, 2, 96), scalar=vcol(l, V_GKSW, 0, 96), in1=sinT[0:96, t0:t0 + 1024],
                                op0=ALU.mult, op1=ALU.mult), reads=rps(2, 2) + [r_tab, r_vecs], writes=[r_tc])
                            pool.op(lambda e: e.tensor_tensor(out=tmpb[0:96, :], in0=tmpb[0:96, :], in1=tmpc[0:96, :], op=ALU.add),
                                    reads=[r_tb, r_tc], writes=[r_tb])
                            pool.op(lambda e: e.tensor_tensor(out=kropeT[0:96, t0:t0 + 1024], in0=tmpb[0:96, :], in1=tmpa[0:96, :],
                                                              op=ALU.mult), reads=[r_tb, r_ta], writes=[r_krope])
                        wrel((l, "lat"))
                        if l == layer_ids[0]:
                            dump("cqn", cqn[:, :, :], [r_cqn], [128, 2, T])
                            dump("ckvn", ckvn[:, :], [r_ckvn], [128, T])
                            dump("kropeT", kropeT[0:96, :], [r_krope], [96, T])
                    barrier()

                    with ExitStack() as s2:
                        QT = sb("QT", [128, T], BF16, s2)
                        KT = sb("KT", [128, T], BF16, s2)
                        r_q, r_k = Reg("QT"), Reg("KT")
                        Vaug = sb("Vaug", [128, 16, 65], BF16, s2)
                        r_v = Reg("Vaug")
                        PT = [sb(f"PT{i}", [128, 1024], BF16, s2) for i in range(3)]
                        r_pt = [Reg(f"PT{i}") for i in range(3)]
                        ta = sb("ta2", [128, 1024], F32, s2)
                        tb_ = sb("tb2", [128, 1024], F32, s2)
                        rq = sb("rq2", [128, 1024], F32, s2)
                        sqa = sb("sqa", [128, 1024], BF16, s2)
                        r_ta, r_tb, r_rq, r_sqa = Reg("ta"), Reg("tb"), Reg("rq"), Reg("sqa")
                        rrow = sb("rrow", [128, 1024], F32, s2)
                        bcs = sb("bcs", [128, 1024], F32, s2)
                        r_rrow, r_bcs = Reg("rrow"), Reg("bcs")
                        yodd = sb("yodd", [128, T], BF16, s2)
                        r_yodd = Reg("yodd")
                        pool.op(lambda e: e.memset(Vaug[:, :, 64:65], 1.0), writes=[r_v])
                        for h in range(NH):
                            for hh in range(2):
                                t0 = hh * 1024
                                mm_group(lambda tb: psb(tb, 1, 96), 96, [wuq[:, k, h * 96:(h + 1) * 96] for k in range(2)],
                                         lambda k, tb: cqn[:, k, t0 + tb * 512:t0 + (tb + 1) * 512], 2,
                                         reads=[r_wsm, r_cqn], writes=rps(0, 2))
                                mm_group(lambda tb: psb(2 + tb, 1, 96), 96, [wuqsw[:, k, h * 96:(h + 1) * 96] for k in range(2)],
                                         lambda k, tb: cqn[:, k, t0 + tb * 512:t0 + (tb + 1) * 512], 2,
                                         reads=[r_wsm, r_cqn], writes=rps(2, 2))
                                act.op(lambda e: e.activation(out=sqa[0:96, :], in_=psb(0, 2, 96), func=AF.Square),
                                       reads=rps(0, 2), writes=[r_sqa])
                                mm_group(lambda tb: psb(4 + tb, 1, 96), 96, [blk_b[0:96, 0:96]],
                                         lambda k, tb: sqa[0:96, tb * 512:(tb + 1) * 512], 2,
                                         reads=[r_sqa, r_const], writes=rps(4, 2))
                                rms_from_psum(rq[0:96, :], r_rq, 4, 2, 96, 1.0)
                                dve.op(lambda e: e.scalar_tensor_tensor(
                                    out=ta[0:96, :], in0=psb(0, 2, 96), scalar=vcol(l, V_GQ, 0, 96), in1=cosT[0:96, t0:t0 + 1024],
                                    op0=ALU.mult, op1=ALU.mult), reads=rps(0, 2) + [r_tab, r_vecs], writes=[r_ta])
                                dve.op(lambda e: e.scalar_tensor_tensor(
                                    out=tb_[0:96, :], in0=psb(2, 2, 96), scalar=vcol(l, V_GQSW, 0, 96), in1=sinT[0:96, t0:t0 + 1024],
                                    op0=ALU.mult, op1=ALU.mult), reads=rps(2, 2) + [r_tab, r_vecs], writes=[r_tb])
                                pool.op(lambda e: e.tensor_tensor(out=ta[0:96, :], in0=ta[0:96, :], in1=tb_[0:96, :], op=ALU.add),
                                        reads=[r_ta, r_tb], writes=[r_ta])
                                pool.op(lambda e: e.tensor_tensor(out=QT[0:96, t0:t0 + 1024], in0=ta[0:96, :], in1=rq[0:96, :],
                                                                  op=ALU.mult), reads=[r_ta, r_rq], writes=[r_q])
                                mm_group(lambda tb: psb(6 + tb, 1, 64), 64, [wukv[:, h * 128:h * 128 + 64]],
                                         lambda k, tb: ckvn[:, t0 + tb * 512:t0 + (tb + 1) * 512], 2,
                                         reads=[r_wsm, r_ckvn], writes=rps(6, 2))
                                act.op(lambda e: e.activation(out=sqa[0:64, :], in_=psb(6, 2, 64), func=AF.Square),
                                       reads=rps(6, 2), writes=[r_sqa])
                                mm_group(lambda tb: psb(4 + tb, 1, 64), 64, [blk_b[0:64, 0:64]],
                                         lambda k, tb: sqa[0:64, tb * 512:(tb + 1) * 512], 2,
                                         reads=[r_sqa, r_const], writes=rps(4, 2))
                                rms_from_psum(rq[0:64, :], r_rq, 4, 2, 64, 1.0)
                                dve.op(lambda e: e.scalar_tensor_tensor(
                                    out=KT[0:64, t0:t0 + 1024], in0=psb(6, 2, 64), scalar=vcol(l, V_GK, 0, 64), in1=rq[0:64, :],
                                    op0=ALU.mult, op1=ALU.mult), reads=rps(6, 2) + [r_rq, r_vecs], writes=[r_k])
                            pool.op(lambda e: e.tensor_copy(out=KT[64:96, :], in_=kropeT[64:96, :]),
                                    reads=[r_krope], writes=[r_k])
                            def vfn(e, h=h):
                                last = None
                                for kc in range(16):
                                    last = e.matmul(ps[:, kc * 64:(kc + 1) * 64], lhsT=ckvn[:, kc * 128:(kc + 1) * 128],
                                                    rhs=wukv[:, h * 128 + 64:h * 128 + 128], start=True, stop=True, skip_group_check=True)
                                return last
                            pe.op(vfn, reads=[r_wsm, r_ckvn], writes=rps(0, 2))
                            act.op(lambda e: e.activation(out=Vaug[:, :, 0:64], in_=ps[:, 0:1024].rearrange("p (a b) -> p a b", b=64),
                                                          func=AF.Copy), reads=rps(0, 2), writes=[r_v])
                            if l == layer_ids[0] and h == 0:
                                dump("QT0", QT[0:96, :], [r_q], [96, T])
                                dump("KT0", KT[0:96, :], [r_k], [96, T])
                                dump("V0", Vaug[:, :, :], [r_v], [128, 16, 65])
                            ydst = yattn[0:64, h // 2, :] if h % 2 == 0 else yodd[0:64, :]
                            r_ydst = r_ya[h // 2] if h % 2 == 0 else r_yodd
                            for qh in range(2):
                                q0 = qh * 1024
                                for kc in range(16):
                                    sbk = (kc % 2) * 2
                                    pb = kc % 3

                                    def sfn(e, kc=kc, sbk=sbk):
                                        last = None
                                        for tb in range(2):
                                            last = e.matmul(psb(sbk + tb), lhsT=KT[0:96, kc * 128:(kc + 1) * 128],
                                                            rhs=QT[0:96, q0 + tb * 512:q0 + (tb + 1) * 512], start=True, stop=True, skip_group_check=True)
                                        return last
                                    pe.op(sfn, reads=[r_k, r_q], writes=rps(sbk, 2))
                                    act.op(lambda e, sbk=sbk, pb=pb: e.activation(out=PT[pb][:, :], in_=psb(sbk, 2), func=AF.Exp, scale=SCALE),
                                           reads=rps(sbk, 2), writes=[r_pt[pb]])

                                    def ofn(e, kc=kc, pb=pb):
                                        last = None
                                        for tb in range(2):
                                            last = e.matmul(psb(4 + tb, 1, 65), lhsT=Vaug[:, kc, 0:65],
                                                            rhs=PT[pb][:, tb * 512:(tb + 1) * 512],
                                                            start=(kc == 0), stop=(kc == 15), skip_group_check=True)
                                        return last
                                    pe.op(ofn, reads=[r_v, r_pt[pb]], writes=rps(4, 2))
                                dve.op(lambda e: e.reciprocal(out=rrow[64:65, :], in_=psb(4, 2, 1, 64)),
                                       reads=rps(4, 2), writes=[r_rrow])
                                mm_group(lambda tb: psb(6 + tb, 1, 64), 64, [ones_f[64:65, 0:64]],
                                         lambda k, tb: rrow[64:65, tb * 512:(tb + 1) * 512], 2,
                                         reads=[r_rrow, r_const], writes=rps(6, 2))
                                act.op(lambda e: e.activation(out=bcs[0:64, :], in_=psb(6, 2, 64), func=AF.Copy),
                                       reads=rps(6, 2), writes=[r_bcs])
                                dve.op(lambda e, q0=q0, ydst=ydst: e.tensor_tensor(out=ydst[:, q0:q0 + 1024], in0=psb(4, 2, 64), in1=bcs[0:64, :],
                                                                                   op=ALU.mult),
                                       reads=rps(4, 2) + [r_bcs], writes=[r_ydst])
                            if h % 2 == 1:
                                sp.dma(yattn[64:128, h // 2, :], yodd[0:64, :], ds_yodd, reads=[r_yodd], writes=[r_ya[h // 2]])
                        if l == layer_ids[0]:
                            dump("yattn", yattn[:, :, :], r_ya, [128, 4, T])
                    barrier()
                barrier()

                with ExitStack() as s3:
                    hT = sb("hT", [128, 8, T], BF16, s3)
                    r_h = [Reg(f"h{c}") for c in range(8)]
                    yconv = sb("yconv", [128, 4, T], BF16, s3)
                    r_yc = [Reg(f"yc{c}") for c in range(4)]
                    make_hT(hT, r_h, l, V_GMIX)
                    with ExitStack() as s4:
                        tA = sb("tA", [128, T], F32, s4)
                        prod = sb("prod", [128, T + 2], F32, s4)
                        r_tA, r_prod = Reg("tA"), Reg("prod")
                        pool.op(lambda e: e.memset(prod[:, 0:1], 0.0), writes=[r_prod])
                        pool.op(lambda e: e.memset(prod[:, T + 1:T + 2], 0.0), writes=[r_prod])
                        wcc, r_wcc = wget((l, "cc"))
                        wcx, r_wcx = wget((l, "cx"))
                        wcb, r_wcb = wget((l, "cb"))
                        for j in range(4):
                            hrhs = lambda k, tb: hT[:, k, tb * 512:(tb + 1) * 512]
                            mm_group(lambda tb: psb(tb), 128, [wcc[:, k, j * 128:(j + 1) * 128] for k in range(8)], hrhs, 4,
                                     reads=[r_wcc] + r_h, writes=rps(0, 4))
                            act.op(lambda e: e.activation(out=tA[:, :], in_=psb(0, 4), func=AF.Copy), reads=rps(0, 4), writes=[r_tA])
                            mm_group(lambda tb: psb(4 + tb), 128, [wcx[:, k, j * 128:(j + 1) * 128] for k in range(8)], hrhs, 4,
                                     reads=[r_wcx] + r_h, writes=rps(4, 4))
                            dve.op(lambda e: e.tensor_tensor(out=prod[:, 1:T + 1], in0=psb(4, 4), in1=tA[:, :], op=ALU.mult),
                                   reads=rps(4, 4) + [r_tA], writes=[r_prod])
                            mm_group(lambda tb: psb(tb), 128, [wcb[:, k, j * 128:(j + 1) * 128] for k in range(8)], hrhs, 4,
                                     reads=[r_wcb] + r_h, writes=rps(0, 4))
                            act.op(lambda e, j=j: e.activation(out=tA[:, :], in_=prod[:, 1:T + 1], func=AF.Copy, scale=vcol(l, V_WC + 4 + j)),
                                   reads=[r_prod, r_vecs], writes=[r_tA])
                            dve.op(lambda e, j=j: e.scalar_tensor_tensor(out=tA[:, :], in0=prod[:, 0:T], scalar=vcol(l, V_WC + j), in1=tA[:, :],
                                                                          op0=ALU.mult, op1=ALU.add), reads=[r_prod, r_tA, r_vecs], writes=[r_tA])
                            dve.op(lambda e, j=j: e.scalar_tensor_tensor(out=tA[:, :], in0=prod[:, 2:T + 2], scalar=vcol(l, V_WC + 8 + j), in1=tA[:, :],
                                                                          op0=ALU.mult, op1=ALU.add), reads=[r_prod, r_tA, r_vecs], writes=[r_tA])
                            dve.op(lambda e, j=j: e.tensor_tensor(out=yconv[:, j, :], in0=psb(0, 4), in1=tA[:, :], op=ALU.mult),
                                   reads=rps(0, 4) + [r_tA], writes=[r_yc[j]])
                        wrel((l, "cc"))
                        wrel((l, "cx"))
                        wrel((l, "cb"))
                        if l == layer_ids[0]:
                            dump("yconv", yconv[:, :, :], r_yc, [128, 4, T])
                    barrier()
                    with ExitStack() as s5:
                        merged = sb("merged", [128, 8, 1024], BF16, s5)
                        r_mg = [Reg(f"mg{c}") for c in range(8)]
                        gaT = sb("gaT", [128, 1024], F32, s5)
                        gcT = sb("gcT", [128, 1024], F32, s5)
                        r_ga, r_gc = Reg("ga"), Reg("gc")
                        wcu = wsmall[:, 0:4096].rearrange("p (a b) -> p a b", b=1024)
                        pool.dma(wcu, w_cu_d[l].rearrange("(kc p) n -> p kc n", p=128), ds_wsm, writes=[r_wsm])
                        for hh in range(2):
                            t0 = hh * 1024
                            wau, r_wau = None, None
                            for j in range(8):
                                jg, jj = j // 4, j % 4
                                wga, r_wga = wget((l, "ga", hh, jg))
                                wgc, r_wgc = wget((l, "gc", hh, jg))
                                if wau is None:
                                    wau, r_wau = wget((l, "wau", hh))
                                hrhs = lambda k, tb: hT[:, k, t0 + tb * 512:t0 + (tb + 1) * 512]
                                mm_group(lambda tb: psb(tb), 128, [wga[:, k, jj * 128:(jj + 1) * 128] for k in range(8)], hrhs, 2,
                                         reads=[r_wga] + r_h, writes=rps(0, 2))
                                act.op(lambda e, j=j: e.activation(out=gaT[:, :], in_=psb(0, 2), func=AF.Sigmoid, bias=vcol(l, V_BG + j)),
                                       reads=rps(0, 2) + [r_vecs], writes=[r_ga])
                                mm_group(lambda tb: psb(2 + tb), 128, [wau[:, k, j * 128:(j + 1) * 128] for k in range(4)],
                                         lambda k, tb: yattn[:, k, t0 + tb * 512:t0 + (tb + 1) * 512], 2,
                                         reads=[r_wau] + r_ya, writes=rps(2, 2))
                                dve.op(lambda e: e.tensor_tensor(out=gaT[:, :], in0=psb(2, 2), in1=gaT[:, :], op=ALU.mult),
                                       reads=rps(2, 2) + [r_ga], writes=[r_ga])
                                mm_group(lambda tb: psb(4 + tb), 128, [wgc[:, k, jj * 128:(jj + 1) * 128] for k in range(8)], hrhs, 2,
                                         reads=[r_wgc] + r_h, writes=rps(4, 2))
                                act.op(lambda e, j=j: e.activation(out=gcT[:, :], in_=psb(4, 2), func=AF.Sigmoid, bias=vcol(l, V_BG + 8 + j)),
                                       reads=rps(4, 2) + [r_vecs], writes=[r_gc])
                                mm_group(lambda tb: psb(6 + tb), 128, [wcu[:, k, j * 128:(j + 1) * 128] for k in range(4)],
                                         lambda k, tb: yconv[:, k, t0 + tb * 512:t0 + (tb + 1) * 512], 2,
                                         reads=[r_wsm] + r_yc, writes=rps(6, 2))
                                dve.op(lambda e: e.tensor_tensor(out=gcT[:, :], in0=psb(6, 2), in1=gcT[:, :], op=ALU.mult),
                                       reads=rps(6, 2) + [r_gc], writes=[r_gc])
                                pool.op(lambda e, j=j: e.tensor_tensor(out=merged[:, j, :], in0=gaT[:, :], in1=gcT[:, :], op=ALU.add),
                                        reads=[r_ga, r_gc], writes=[r_mg[j]])
                                if jj == 3:
                                    wrel((l, "ga", hh, jg))
                                    wrel((l, "gc", hh, jg))
                            wrel((l, "wau", hh))
                            for m in range(8):
                                mg, mm = m // 4, m % 4
                                wo, r_wo = wget((l, "wo", hh, mg))
                                b0 = (m % 4) * 2
                                mm_group(lambda tb, b0=b0: psb(b0 + tb), 128, [wo[:, k, mm * 128:(mm + 1) * 128] for k in range(8)],
                                         lambda k, tb: merged[:, k, tb * 512:(tb + 1) * 512], 2,
                                         reads=[r_wo] + r_mg, writes=rps(b0, 2))
                                dve.op(lambda e, m=m, b0=b0: e.tensor_tensor(out=xT[:, m, t0:t0 + 1024], in0=psb(b0, 2), in1=xT[:, m, t0:t0 + 1024],
                                                                              op=ALU.add), reads=rps(b0, 2) + [r_x[m]], writes=[r_x[m]])
                                if mm == 3:
                                    wrel((l, "wo", hh, mg))
                    barrier()
                barrier()
            barrier()
            if l == layer_ids[0]:
                dump("x_mix", xT[:, :, :], r_x, [128, 8, T])

            with ExitStack() as s6:
                hT = sb("hT", [128, 8, T], BF16, s6)
                r_h = [Reg(f"h{c}") for c in range(8)]
                actT = sb("actT", [128, FFN_MAXCH, T], BF16, s6)
                r_actc = [Reg(f"act{c}") for c in range(FFN_MAXCH)]
                u0 = sb("u0", [128, T], F32, s6)
                u1 = sb("u1", [128, T], F32, s6)
                sT = sb("sT", [128, T], F32, s6)
                r_u = [Reg("u0"), Reg("u1")]
                r_s = Reg("sT")
                norm_stats([actT[:, 0, :], actT[:, 1, :]], [r_actc[0], r_actc[1]], ones_b[:, :])
                make_hT(hT, r_h, l, V_GFFN)
                hrhs = lambda k, tb: hT[:, k, tb * 512:(tb + 1) * 512]
                for g, (p_lo, p_hi) in enumerate(FFN_GROUPS):
                    chunks = []
                    for pp in range(p_lo, p_hi):
                        wa, r_wa = wget((l, "ua", pp))
                        wv, r_wv = wget((l, "uv", pp))
                        npair = 4 if pp < 5 else 2
                        for jj in range(npair):
                            j = pp * 4 + jj
                            ci = len(chunks)
                            chunks.append(j)
                            for which, (wp, r_wp, half, ccol) in enumerate(((wa, r_wa, 0, j), (wv, r_wv, 1, 22 + j))):
                                b0 = half * 4
                                u = (u0, u1)[which]
                                r_uu = r_u[which]
                                mm_group(lambda tb, b0=b0: psb(b0 + tb), 128, [wp[:, k, jj * 128:(jj + 1) * 128] for k in range(8)], hrhs, 4,
                                         reads=[r_wp] + r_h, writes=rps(b0, 4))
                                act.op(lambda e, u=u, b0=b0, ccol=ccol: e.activation(out=u[:, :], in_=psb(b0, 4), func=AF.Copy,
                                                                                    scale=vcol(l, V_WFC + 44 + ccol)),
                                       reads=rps(b0, 4) + [r_vecs], writes=[r_uu])
                                dve.op(lambda e, u=u, b0=b0, ccol=ccol: e.scalar_tensor_tensor(
                                    out=u[:, 1:T], in0=ps[:, b0 * 512:b0 * 512 + T - 1], scalar=vcol(l, V_WFC + ccol), in1=u[:, 1:T],
                                    op0=ALU.mult, op1=ALU.add), reads=rps(b0, 4) + [r_uu, r_vecs], writes=[r_uu])
                                dve.op(lambda e, u=u, b0=b0, ccol=ccol: e.scalar_tensor_tensor(
                                    out=u[:, 0:T - 1], in0=ps[:, b0 * 512 + 1:b0 * 512 + T], scalar=vcol(l, V_WFC + 88 + ccol), in1=u[:, 0:T - 1],
                                    op0=ALU.mult, op1=ALU.add), reads=rps(b0, 4) + [r_uu, r_vecs], writes=[r_uu])
                                if which == 0:
                                    act.op(lambda e: e.activation(out=sT[:, :], in_=u0[:, :], func=AF.Silu), reads=[r_u[0]], writes=[r_s])
                                else:
                                    pool.op(lambda e, ci=ci: e.tensor_tensor(out=actT[:, ci, :], in0=sT[:, :], in1=u1[:, :], op=ALU.mult),
                                            reads=[r_s, r_u[1]], writes=[r_actc[ci]])
                        wrel((l, "ua", pp))
                        wrel((l, "uv", pp))
                    if l == layer_ids[0] and g == 0:
                        dump("act0", actT[:, 0, :], [r_actc[0]], [128, T])
                    wds = []
                    for pp in range(p_lo, p_hi):
                        wds.append(wget((l, "dn", pp)))
                    nch = len(chunks)
                    for m in range(8):
                        b0 = (m % 2) * 4
                        lhs = [wds[ci // 4][0][:, ci % 4, m * 128:(m + 1) * 128] for ci in range(nch)]
                        mm_group(lambda tb, b0=b0: psb(b0 + tb), 128, lhs,
                                 lambda k, tb: actT[:, k, tb * 512:(tb + 1) * 512], 4,
                                 reads=[w[1] for w in wds] + r_actc[0:nch], writes=rps(b0, 4))
                        dve.op(lambda e, m=m, b0=b0: e.tensor_tensor(out=xT[:, m, :], in0=psb(b0, 4), in1=xT[:, m, :], op=ALU.add),
                               reads=rps(b0, 4) + [r_x[m]], writes=[r_x[m]])
                    for pp in range(p_lo, p_hi):
                        wrel((l, "dn", pp))
            barrier()
            if l == layer_ids[0]:
                dump("x_ffn", xT[:, :, :], r_x, [128, 8, T])

            with ExitStack() as s7:
                hT = sb("hT", [128, 8, T], BF16, s7)
                r_h = [Reg(f"h{c}") for c in range(8)]
                pTb = sb("pTb", [128, 2, T], BF16, s7)
                r_p = Reg("pT")
                pg0 = sb("pg0", [128, T], F32, s7)
                pg1 = sb("pg1", [128, T], F32, s7)
                r_pg = [Reg("pg0"), Reg("pg1")]
                sq0 = sb("sq0b", [128, T], BF16, s7)
                sq1 = sb("sq1b", [128, T], BF16, s7)
                pool.dma(pTb[:, :, :], pT_d[l].rearrange("(kc p) t -> p kc t", p=128), ds_p, writes=[r_p])
                norm_stats([sq0, sq1], [Reg("sq0"), Reg("sq1")], ones_b[:, :])
                make_hT(hT, r_h, l, V_GPLE)
                hrhs = lambda k, tb: hT[:, k, tb * 512:(tb + 1) * 512]
                wpl, r_wpl = wget((l, "pl"))
                for m in range(8):
                    mg, mm = m // 4, m % 4
                    wpg, r_wpg = wget((l, "pg", mg))
                    pg = (pg0, pg1)[m % 2]
                    r_pgm = r_pg[m % 2]
                    mm_group(lambda tb: psb(tb), 128, [wpg[:, k, mm * 128:(mm + 1) * 128] for k in range(8)], hrhs, 4,
                             reads=[r_wpg] + r_h, writes=rps(0, 4))
                    act.op(lambda e, pg=pg: e.activation(out=pg[:, :], in_=psb(0, 4), func=AF.Sigmoid), reads=rps(0, 4), writes=[r_pgm])
                    mm_group(lambda tb: psb(4 + tb), 128, [wpl[:, k, m * 128:(m + 1) * 128] for k in range(2)],
                             lambda k, tb: pTb[:, k, tb * 512:(tb + 1) * 512], 4,
                             reads=[r_wpl, r_p], writes=rps(4, 4))
                    dve.op(lambda e, pg=pg: e.tensor_tensor(out=pg[:, :], in0=psb(4, 4), in1=pg[:, :], op=ALU.mult),
                           reads=rps(4, 4) + [r_pgm], writes=[r_pgm])
                    pool.op(lambda e, pg=pg, m=m: e.tensor_tensor(out=xT[:, m, :], in0=xT[:, m, :], in1=pg[:, :], op=ALU.add),
                            reads=[r_pgm, r_x[m]], writes=[r_x[m]])
                    if mm == 3:
                        wrel((l, "pg", mg))
                wrel((l, "pl"))
            barrier()

        for c in range(8):
            sp.dma(outT_d[c * 128:(c + 1) * 128, :], xT[:, c, :], ds_out, reads=[r_x[c]])
        sp.wait(Ev(ds_out[0], ds_out[1]))
        if dbg and ds_dbg[1] > 0:
            sp.wait(Ev(ds_dbg[0], ds_dbg[1]))
        for e in engines:
            if e.seq > 0 and e is not sp:
                sp.wait(Ev(e.sem, e.seq))
        for ds in all_ds:
            if ds[1] > 0:
                sp.wait(Ev(ds[0], ds[1]))

        with nc.Block() as block:
            @block.tensor
            def _(e):
                for t in pe.thunks:
                    t(e)

            @block.scalar
            def _(e):
                for t in act.thunks:
                    t(e)

            @block.vector
            def _(e):
                for t in dve.thunks:
                    t(e)

            @block.gpsimd
            def _(e):
                for t in pool.thunks:
                    t(e)

            @block.sync
            def _(e):
                for t in sp.thunks:
                    t(e)
    return dbg_out


FFN_GROUPS = [(0, 2), (2, 4), (4, 6)]
FFN_MAXCH = 8


def _pack_vecs(inp, l):
    v = np.zeros((128, NV), np.float32)
    v[:, V_GMIX:V_GMIX + 8] = inp["g_mix"][l].reshape(8, 128).T
    v[:, V_GFFN:V_GFFN + 8] = inp["g_ffn"][l].reshape(8, 128).T
    v[:, V_GPLE:V_GPLE + 8] = inp["g_ple"][l].reshape(8, 128).T
    v[:, V_GQL:V_GQL + 2] = inp["g_q_lat"][l].reshape(2, 128).T
    v[:, V_GKVL] = inp["g_kv_lat"][l]
    v[:, V_BG:V_BG + 16] = inp["b_gate"][l].reshape(16, 128).T
    wc = inp["w_conv"][l]
    for k in range(3):
        v[:, V_WC + k * 4:V_WC + k * 4 + 4] = wc[k].reshape(4, 128).T
    wf = inp["w_ffn_conv"][l]
    for k in range(3):
        v[:, V_WFC + k * 44:V_WFC + k * 44 + 44] = wf[k].reshape(44, 128).T
    sw = np.concatenate([np.arange(64), 80 + np.arange(16), 64 + np.arange(16)])
    gq = inp["g_q_head"][l]
    gk = inp["g_k_head"][l]
    v[0:96, V_GQ] = gq
    v[0:96, V_GQSW] = gq[sw]
    v[0:96, V_GK] = gk
    v[0:96, V_GKSW] = gk[sw]
    inv = (10000.0 ** (-np.arange(0, 32, 2, dtype=np.float32) / np.float32(32))).astype(np.float32)
    v[64:96, V_INV] = np.concatenate([inv, inv])
    v[64:80, V_SGN] = -1.0
    v[80:96, V_SGN] = 1.0
    return v


def _prep_shared(inp):
    f = lambda a: np.ascontiguousarray(np.asarray(a, dtype=np.float32))
    w_in = f(inp["w_in"])
    L = w_in.shape[0]
    w_kr = np.zeros((L, D, 96), np.float32)
    w_kr[:, :, 64:96] = w_in[:, :, 384:416]
    w_kr_sw = np.zeros((L, D, 96), np.float32)
    w_kr_sw[:, :, 64:80] = w_in[:, :, 400:416]
    w_kr_sw[:, :, 80:96] = w_in[:, :, 384:400]
    w_uq = f(inp["w_uq"])
    idx = np.arange(NH * DQK).reshape(NH, DQK).copy()
    sw = np.concatenate([np.arange(64), 80 + np.arange(16), 64 + np.arange(16)])
    idx = idx[:, sw].reshape(-1)
    w_uq_sw = np.ascontiguousarray(w_uq[:, :, idx])
    vecs = np.concatenate([_pack_vecs(inp, l) for l in range(L)], axis=1)
    return {
        "vecs": np.ascontiguousarray(vecs),
        "w_in": w_in, "w_kr": w_kr, "w_kr_sw": w_kr_sw, "w_uq": w_uq, "w_uq_sw": w_uq_sw,
        "w_ukv": f(inp["w_ukv"]), "w_attn_up": f(inp["w_attn_up"]), "w_conv_up": f(inp["w_conv_up"]),
        "w_o": f(inp["w_o"]), "w_up": f(inp["w_up"]), "w_down": f(inp["w_down"]),
        "w_ple_gate": f(inp["w_ple_gate"]), "w_ple": f(inp["w_ple"]),
    }


def _core_inputs(inp, shared, b):
    x = np.asarray(inp["x"], dtype=np.float32)
    p = np.asarray(inp["p"], dtype=np.float32)
    pos = np.asarray(inp["positions"]).astype(np.int32)
    m = dict(shared)
    m["xT"] = np.ascontiguousarray(x[b].T)
    m["pT"] = np.ascontiguousarray(np.transpose(p[:, b], (0, 2, 1)))
    m["pos"] = np.ascontiguousarray(pos[b][None, :])
    return m


def kernel(**inputs):
    shared = _prep_shared(inputs)
    in_maps = [_core_inputs(inputs, shared, b) for b in range(NCORES)]
    nc = bass.Bass("TRN2", target_bir_lowering=False)
    build_program(nc, list(range(DEPTH)))
    res = run_bass_kernel_spmd(nc, in_maps, core_ids=list(range(NCORES)))
    out = np.stack([np.ascontiguousarray(res.results[b]["outT"].T) for b in range(NCORES)], axis=0)
    return out.astype(np.float32)
```

```python
import numpy as np
from contextlib import ExitStack
import concourse.bass as bass
import concourse.mybir as mybir
from concourse.bass_utils import run_bass_kernel_spmd

F32 = mybir.dt.float32
BF16 = mybir.dt.bfloat16
I32 = mybir.dt.int32
AF = mybir.ActivationFunctionType
ALU = mybir.AluOpType

DEPTH = 2
D = 1024
T = 2048
NH = 8
DQK = 96
QL = 256
KVL = 128
DFF = 2816
INC = 4000
EPS = 1e-6
NCORES = 8
SCALE = float(DQK ** -0.5)
MAGIC = 12582912.0

V_GMIX, V_GFFN, V_GPLE = 0, 8, 16
V_GQL, V_GKVL = 24, 26
V_BG = 27
V_WC = 43
V_WFC = 55
V_GQ, V_GQSW, V_GK, V_GKSW = 187, 188, 189, 190
V_INV, V_SGN = 191, 192
NV = 196


class Ev:
    __slots__ = ("sem", "val")

    def __init__(self, sem, val):
        self.sem = sem
        self.val = val


class Reg:
    __slots__ = ("name", "w", "r")

    def __init__(self, name=""):
        self.name = name
        self.w = None
        self.r = []


class _Rec:
    def __init__(self):
        self.calls = []

    def __getattr__(self, name):
        def f(*a, **k):
            self.calls.append((name, a, k))
            return None
        return f


class Eng:
    def __init__(self, name, sem, self_sync=True):
        self.name = name
        self.sem = sem
        self.seq = 0
        self.seen = {}
        self.thunks = []
        self.self_sync = self_sync

    def wait(self, ev):
        if ev is None:
            return
        if (not self.self_sync) and ev.sem is self.sem:
            return
        k = id(ev.sem)
        if self.seen.get(k, 0) >= ev.val:
            return
        self.seen[k] = ev.val
        sem, val = ev.sem, ev.val
        self.thunks.append(lambda e: e.wait_ge(sem, val))

    def deps(self, reads, writes):
        for r in reads:
            self.wait(r.w)
        for w in writes:
            self.wait(w.w)
            for e in w.r:
                self.wait(e)

    def _mark(self, ev, reads, writes):
        for r in reads:
            r.r.append(ev)
            if len(r.r) > 64:
                r.r = r.r[-48:]
        for w in writes:
            w.w = ev
            w.r = []

    def op(self, fn, reads=(), writes=()):
        self.deps(reads, writes)
        self.seq += 1
        sem = self.sem
        rec = _Rec()
        fn(rec)
        calls = rec.calls
        assert calls

        def thunk(e, calls=calls, sem=sem):
            inst = None
            for name, a, k in calls:
                inst = getattr(e, name)(*a, **k)
            inst.then_inc(sem, 1)
        self.thunks.append(thunk)
        ev = Ev(sem, self.seq)
        self._mark(ev, reads, writes)
        return ev

    def dma(self, out, in_, ds, reads=(), writes=(), mark=True):
        self.deps(reads, writes)
        ds[1] += 16
        sem = ds[0]
        self.thunks.append(lambda e: e.dma_start(out=out, in_=in_).then_inc(sem, 16))
        ev = Ev(sem, ds[1])
        if mark:
            self._mark(ev, reads, writes)
        return ev


def build_program(nc, layer_ids, dbg=False):
    L = DEPTH
    dram_in = lambda name, shape, dt=F32: nc.dram_tensor(name, list(shape), dt, kind="ExternalInput").ap()
    xT_d = dram_in("xT", [D, T])
    pT_d = dram_in("pT", [L, 256, T])
    pos_d = dram_in("pos", [1, T], I32)
    vecs_d = dram_in("vecs", [128, L * NV])
    w_in_d = dram_in("w_in", [L, D, INC])
    w_kr_d = dram_in("w_kr", [L, D, 96])
    w_krsw_d = dram_in("w_kr_sw", [L, D, 96])
    w_uq_d = dram_in("w_uq", [L, QL, NH * DQK])
    w_uqsw_d = dram_in("w_uq_sw", [L, QL, NH * DQK])
    w_ukv_d = dram_in("w_ukv", [L, KVL, NH * 128])
    w_au_d = dram_in("w_attn_up", [L, 512, D])
    w_cu_d = dram_in("w_conv_up", [L, 512, D])
    w_o_d = dram_in("w_o", [L, D, D])
    w_up_d = dram_in("w_up", [L, D, 2 * DFF])
    w_dn_d = dram_in("w_down", [L, DFF, D])
    w_pg_d = dram_in("w_ple_gate", [L, D, D])
    w_pl_d = dram_in("w_ple", [L, 256, D])
    outT_d = nc.dram_tensor("outT", [D, T], F32, kind="ExternalOutput").ap()
    dbg_out = {}

    with ExitStack() as st:
        _cnt = [0]

        def sb(name, shape, dt, stack=st):
            _cnt[0] += 1
            return stack.enter_context(nc.sbuf_tensor(f"sb{_cnt[0]}_{name}", list(shape), dt))

        def new_sem(name):
            return st.enter_context(nc.semaphore(name))

        pe = Eng("pe", new_sem("s_pe"), self_sync=False)
        act = Eng("act", new_sem("s_act"))
        dve = Eng("dve", new_sem("s_dve"))
        pool = Eng("pool", new_sem("s_pool"))
        sp = Eng("sp", new_sem("s_sp"))
        engines = [pe, act, dve, pool, sp]
        all_ds = []

        def new_ds(name):
            ds = [new_sem(name), 0]
            all_ds.append(ds)
            return ds

        def barrier():
            for e in engines:
                for x in engines:
                    if x is not e and x.seq > 0:
                        e.wait(Ev(x.sem, x.seq))
                for ds in all_ds:
                    if ds[1] > 0:
                        e.wait(Ev(ds[0], ds[1]))

        xT = sb("xT", [128, 8, T], F32)
        r_x = [Reg(f"x{c}") for c in range(8)]
        rstd = sb("rstd", [128, T], F32)
        r_rstd = Reg("rstd")
        vecs = sb("vecs", [128, L * NV], F32)
        r_vecs = Reg("vecs")
        ones_b = sb("ones_b", [128, 128], BF16)
        blk_b = sb("blk_b", [128, 96], BF16)
        ones_f = sb("ones_f", [128, 64], F32)
        epsc = sb("epsc", [128, 1], F32)
        r_const = Reg("const")
        NS = 3
        SLOT = 4096
        wring = [sb(f"wring{i}", [128, SLOT], BF16) for i in range(NS)]
        r_ring = [Reg(f"ring{i}") for i in range(NS)]
        ds_ring = [new_ds(f"d_ring{i}") for i in range(NS)]
        WSM = 5632
        wsmall = sb("wsmall", [128, WSM], BF16)
        r_wsm = Reg("wsmall")
        ds_wsm = new_ds("d_wsm")
        ps = st.enter_context(nc.psum_tensor("ps", [128, 8 * 512], F32))
        r_ps = [Reg(f"ps{i}") for i in range(8)]

        def psb(b0, nb=1, m=128, p0=0):
            return ps[p0:p0 + m, b0 * 512:(b0 + nb) * 512]

        def rps(b0, nb=1):
            return r_ps[b0:b0 + nb]

        ds_x = [new_ds(f"d_x{c}") for c in range(8)]
        ds_misc = new_ds("d_misc")
        ds_pos = new_ds("d_pos")
        ds_p = new_ds("d_p")
        ds_yodd = new_ds("d_yodd")
        ds_out = new_ds("d_out")
        ds_dbg = new_ds("d_dbg")

        def vcol(l, col, p0=0, p1=128, n=1):
            return vecs[p0:p1, l * NV + col:l * NV + col + n]

        def dump(name, ap, reg_list, shape):
            if not dbg:
                return
            t = nc.dram_tensor("dbg_" + name, list(shape), ap.dtype, kind="ExternalOutput").ap()
            dbg_out[name] = t
            sp.dma(t, ap, ds_dbg, reads=reg_list)

        requests = []
        req_index = {}
        state = {"issued": 0, "released": set()}

        def add_req(key, ap, kc, n):
            assert kc * n <= SLOT, (key, kc, n)
            req_index[key] = len(requests)
            requests.append((key, ap, kc, n))

        def panel_rows(w2d, r0, kc, c0, n):
            return w2d[r0:r0 + kc * 128, c0:c0 + n].rearrange("(kc p) n -> p kc n", p=128)

        for l in layer_ids:
            wi = w_in_d[l]
            add_req((l, "lat"), panel_rows(wi, 0, 8, 0, 416), 8, 416)
            add_req((l, "cc"), panel_rows(wi, 0, 8, 928, 512), 8, 512)
            add_req((l, "cx"), panel_rows(wi, 0, 8, 1440, 512), 8, 512)
            add_req((l, "cb"), panel_rows(wi, 0, 8, 416, 512), 8, 512)
            for hh in range(2):
                for jg in range(2):
                    add_req((l, "ga", hh, jg), panel_rows(wi, 0, 8, 1952 + jg * 512, 512), 8, 512)
                    add_req((l, "gc", hh, jg), panel_rows(wi, 0, 8, 2976 + jg * 512, 512), 8, 512)
                    if jg == 0:
                        add_req((l, "wau", hh), panel_rows(w_au_d[l], 0, 4, 0, 1024), 4, 1024)
                for mg in range(2):
                    add_req((l, "wo", hh, mg), panel_rows(w_o_d[l], 0, 8, mg * 512, 512), 8, 512)
            for g, (p_lo, p_hi) in enumerate(FFN_GROUPS):
                for pp in range(p_lo, p_hi):
                    n = 512 if pp < 5 else 256
                    add_req((l, "ua", pp), panel_rows(w_up_d[l], 0, 8, pp * 512, n), 8, n)
                    add_req((l, "uv", pp), panel_rows(w_up_d[l], 0, 8, DFF + pp * 512, n), 8, n)
                for pp in range(p_lo, p_hi):
                    kc = 4 if pp < 5 else 2
                    add_req((l, "dn", pp), panel_rows(w_dn_d[l], pp * 512, kc, 0, 1024), kc, 1024)
            add_req((l, "pl"), panel_rows(w_pl_d[l], 0, 2, 0, 1024), 2, 1024)
            for mg in range(2):
                add_req((l, "pg", mg), panel_rows(w_pg_d[l], 0, 8, mg * 512, 512), 8, 512)

        def pump(upto):
            while state["issued"] < len(requests) and state["issued"] <= upto:
                i = state["issued"]
                if i >= NS and (i - NS) not in state["released"]:
                    break
                key, ap, kc, n = requests[i]
                s = i % NS
                view = wring[s][:, 0:kc * n].rearrange("p (a b) -> p a b", b=n)
                pool.dma(view, ap, ds_ring[s], writes=[r_ring[s]])
                state["issued"] += 1

        def wget(key):
            i = req_index[key]
            pump(i + NS - 1)
            assert state["issued"] > i, ("ring deadlock", key)
            key, ap, kc, n = requests[i]
            s = i % NS
            view = wring[s][:, 0:kc * n].rearrange("p (a b) -> p a b", b=n)
            return view, r_ring[s]

        def wrel(key):
            i = req_index[key]
            state["released"].add(i)
            pump(i + NS)

        def mm_group(out_fn, M, lhs_list, rhs_fn, ntb, reads, writes, p0=0):
            nk = len(lhs_list)

            def fn(e):
                for tb in range(ntb):
                    for k in range(nk):
                        e.matmul(out_fn(tb), lhsT=lhs_list[k], rhs=rhs_fn(k, tb),
                                 start=(k == 0), stop=(k == nk - 1), skip_group_check=True)
            return pe.op(fn, reads=reads, writes=writes)

        def norm_stats(sq_bufs, r_sq, ones_lhs):
            for c in range(8):
                b = c % 2
                act.op(lambda e, c=c, b=b: e.activation(out=sq_bufs[b][:, :], in_=xT[:, c, :], func=AF.Square),
                       reads=[r_x[c]], writes=[r_sq[b]])

                def fn(e, c=c, b=b):
                    last = None
                    for tb in range(4):
                        last = e.matmul(psb(tb), lhsT=ones_lhs, rhs=sq_bufs[b][:, tb * 512:(tb + 1) * 512],
                                        start=(c == 0), stop=(c == 7), skip_group_check=True)
                    return last
                pe.op(fn, reads=[r_sq[b], r_const], writes=rps(0, 4))
            act.op(lambda e: e.activation(out=rstd[:, :], in_=psb(0, 4), func=AF.Sqrt, bias=epsc[:, 0:1], scale=1.0 / D),
                   reads=rps(0, 4) + [r_const], writes=[r_rstd])
            dve.op(lambda e: e.reciprocal(out=rstd[:, :], in_=rstd[:, :]), reads=[r_rstd], writes=[r_rstd])

        def make_hT(hT, r_h, l, gcol):
            for c in range(8):
                eng = dve
                eng.op(lambda e, c=c: e.scalar_tensor_tensor(out=hT[:, c, :], in0=xT[:, c, :],
                                                             scalar=vcol(l, gcol + c), in1=rstd[:, :],
                                                             op0=ALU.mult, op1=ALU.mult),
                       reads=[r_x[c], r_rstd, r_vecs], writes=[r_h[c]])

        def rms_from_psum(dst, r_dst, src_banks, nb, m, scale):
            act.op(lambda e: e.activation(out=dst, in_=psb(src_banks, nb, m), func=AF.Sqrt, bias=epsc[0:m, 0:1], scale=scale),
                   reads=rps(src_banks, nb) + [r_const], writes=[r_dst])
            dve.op(lambda e: e.reciprocal(out=dst, in_=dst), reads=[r_dst], writes=[r_dst])

        for c in range(8):
            sp.dma(xT[:, c, :], xT_d[c * 128:(c + 1) * 128, :], ds_x[c], writes=[r_x[c]])
        sp.dma(vecs[:, :], vecs_d[:, :], ds_misc, writes=[r_vecs])
        pool.op(lambda e: e.memset(ones_b[:, :], 1.0), writes=[r_const])
        pool.op(lambda e: e.memset(blk_b[:, :], 0.0), writes=[r_const])
        pool.op(lambda e: e.memset(blk_b[0:64, 0:64], 1.0 / 64), writes=[r_const])
        pool.op(lambda e: e.memset(blk_b[64:96, 64:96], 1.0 / 32), writes=[r_const])
        pool.op(lambda e: e.memset(ones_f[:, :], 1.0), writes=[r_const])
        pool.op(lambda e: e.memset(epsc[:, :], EPS), writes=[r_const])

        for l in layer_ids:
            wuq = wsmall[:, 0:1536].rearrange("p (a b) -> p a b", b=768)
            wuqsw = wsmall[:, 1536:3072].rearrange("p (a b) -> p a b", b=768)
            wukv = wsmall[:, 3072:4096]
            wkr = wsmall[:, 4096:4864].rearrange("p (a b) -> p a b", b=96)
            wkrsw = wsmall[:, 4864:5632].rearrange("p (a b) -> p a b", b=96)
            pool.dma(wuq, w_uq_d[l].rearrange("(kc p) n -> p kc n", p=128), ds_wsm, writes=[r_wsm], mark=False)
            pool.dma(wuqsw, w_uqsw_d[l].rearrange("(kc p) n -> p kc n", p=128), ds_wsm, mark=False)
            pool.dma(wukv, w_ukv_d[l], ds_wsm, mark=False)
            pool.dma(wkr, w_kr_d[l].rearrange("(kc p) n -> p kc n", p=128), ds_wsm, mark=False)
            ev = pool.dma(wkrsw, w_krsw_d[l].rearrange("(kc p) n -> p kc n", p=128), ds_wsm, mark=False)
            r_wsm.w = ev
            r_wsm.r = []

            with ExitStack() as sL:
                yattn = sb("yattn", [128, 4, T], BF16, sL)
                r_ya = [Reg(f"ya{c}") for c in range(4)]
                with ExitStack() as sA:
                    cqn = sb("cqn", [128, 2, T], BF16, sA)
                    r_cqn = Reg("cqn")
                    ckvn = sb("ckvn", [128, T], BF16, sA)
                    r_ckvn = Reg("ckvn")
                    kropeT = sb("kropeT", [128, T], BF16, sA)
                    r_krope = Reg("krope")
                    cosT = sb("cosT", [128, T], F32, sA)
                    sinT = sb("sinT", [128, T], F32, sA)
                    r_tab = Reg("tab")
                    with ExitStack() as s0:
                        pos_i = sb("pos_i", [128, T], I32, s0)
                        r_pos = Reg("pos")
                        ang = sb("ang", [128, T], F32, s0)
                        r_ang = Reg("ang")
                        sp.dma(pos_i[64:96, :], pos_d[0:1, :].partition_broadcast(32), ds_pos, writes=[r_pos])
                        R = slice(64, 96)
                        dve.op(lambda e: e.tensor_copy(out=ang[R, :], in_=pos_i[R, :]), reads=[r_pos], writes=[r_ang])
                        dve.op(lambda e: e.tensor_scalar(out=ang[R, :], in0=ang[R, :], scalar1=vcol(l, V_INV, 64, 96), scalar2=None,
                                                         op0=ALU.mult), reads=[r_ang, r_vecs], writes=[r_ang])
                        for which, dst, off in (("sin", sinT, 0.0), ("cos", cosT, float(np.pi / 2))):
                            dve.op(lambda e, dst=dst, off=off: e.tensor_scalar(
                                out=dst[R, :], in0=ang[R, :], scalar1=off, scalar2=float(1.0 / (2 * np.pi)),
                                op0=ALU.add, op1=ALU.mult), reads=[r_ang], writes=[r_tab])
                            dve.op(lambda e, dst=dst: e.tensor_scalar(out=dst[R, :], in0=dst[R, :], scalar1=MAGIC, scalar2=None,
                                                                      op0=ALU.add), reads=[r_tab], writes=[r_tab])
                            dve.op(lambda e, dst=dst: e.tensor_scalar(out=dst[R, :], in0=dst[R, :], scalar1=MAGIC, scalar2=None,
                                                                      op0=ALU.subtract), reads=[r_tab], writes=[r_tab])
                            dve.op(lambda e, dst=dst: e.scalar_tensor_tensor(
                                out=dst[R, :], in0=dst[R, :], scalar=float(-2 * np.pi), in1=ang[R, :],
                                op0=ALU.mult, op1=ALU.add), reads=[r_tab, r_ang], writes=[r_tab])
                            dve.op(lambda e, dst=dst, off=off: e.tensor_scalar(
                                out=dst[R, :], in0=dst[R, :], scalar1=off, scalar2=float(np.pi),
                                op0=ALU.add, op1=ALU.min), reads=[r_tab], writes=[r_tab])
                            dve.op(lambda e, dst=dst: e.tensor_scalar(out=dst[R, :], in0=dst[R, :], scalar1=float(-np.pi), scalar2=None,
                                                                      op0=ALU.max), reads=[r_tab], writes=[r_tab])
                            act.op(lambda e, dst=dst: e.activation(out=dst[R, :], in_=dst[R, :], func=AF.Sin),
                                   reads=[r_tab], writes=[r_tab])
                        dve.op(lambda e: e.tensor_scalar(out=sinT[R, :], in0=sinT[R, :], scalar1=vcol(l, V_SGN, 64, 96), scalar2=None,
                                                         op0=ALU.mult), reads=[r_tab, r_vecs], writes=[r_tab])
                        pool.op(lambda e: e.memset(cosT[0:64, :], 1.0), writes=[r_tab])
                        pool.op(lambda e: e.memset(sinT[0:64, :], 0.0), writes=[r_tab])
                        if l == layer_ids[0]:
                            dump("cosT", cosT[0:96, :], [r_tab], [96, T])
                            dump("sinT", sinT[0:96, :], [r_tab], [96, T])
                    barrier()
                    with ExitStack() as s1:
                        hT = sb("hT", [128, 8, T], BF16, s1)
                        r_h = [Reg(f"h{c}") for c in range(8)]
                        sq0 = sb("sq0", [128, T], BF16, s1)
                        r_sq = [Reg("sq0"), Reg("sq0")]
                        r_sq[1] = r_sq[0]
                        tmpa = sb("tmpa", [128, 1024], F32, s1)
                        tmpb = sb("tmpb", [128, 1024], F32, s1)
                        tmpc = sb("tmpc", [128, 1024], F32, s1)
                        r_ta, r_tb, r_tc = Reg("ta"), Reg("tb"), Reg("tc")
                        norm_stats([sq0, sq0], r_sq, ones_b[:, :])
                        make_hT(hT, r_h, l, V_GMIX)
                        if l == layer_ids[0]:
                            dump("hT", hT[:, :, :], r_h, [128, 8, T])

                        wlat, r_wlat = wget((l, "lat"))
                        sqh = sq0[:, 0:2048].rearrange("p (a b) -> p a b", b=1024)
                        for hh in range(2):
                            t0 = hh * 1024
                            for c in range(2):
                                mm_group(lambda tb, c=c: psb(2 * c + tb), 128,
                                         [wlat[:, k, c * 128:(c + 1) * 128] for k in range(8)],
                                         lambda k, tb: hT[:, k, t0 + tb * 512:t0 + (tb + 1) * 512], 2,
                                         reads=[r_wlat] + r_h, writes=rps(2 * c, 2))
                                act.op(lambda e, c=c: e.activation(out=sqh[:, c, :], in_=psb(2 * c, 2), func=AF.Square),
                                       reads=rps(2 * c, 2), writes=[r_sq[0]])
                            mm_group(lambda tb: psb(4 + tb), 128, [ones_b[:, :], ones_b[:, :]],
                                     lambda k, tb: sqh[:, k, tb * 512:(tb + 1) * 512], 2,
                                     reads=[r_sq[0], r_const], writes=rps(4, 2))
                            rms_from_psum(tmpa[:, :], r_ta, 4, 2, 128, 1.0 / QL)
                            for c in range(2):
                                dve.op(lambda e, c=c: e.scalar_tensor_tensor(
                                    out=cqn[:, c, t0:t0 + 1024], in0=psb(2 * c, 2), scalar=vcol(l, V_GQL + c), in1=tmpa[:, :],
                                    op0=ALU.mult, op1=ALU.mult), reads=rps(2 * c, 2) + [r_ta, r_vecs], writes=[r_cqn])
                            mm_group(lambda tb: psb(tb), 128, [wlat[:, k, 256:384] for k in range(8)],
                                     lambda k, tb: hT[:, k, t0 + tb * 512:t0 + (tb + 1) * 512], 2,
                                     reads=[r_wlat] + r_h, writes=rps(0, 2))
                            act.op(lambda e: e.activation(out=sqh[:, 0, :], in_=psb(0, 2), func=AF.Square),
                                   reads=rps(0, 2), writes=[r_sq[0]])
                            mm_group(lambda tb: psb(4 + tb), 128, [ones_b[:, :]],
                                     lambda k, tb: sqh[:, 0, tb * 512:(tb + 1) * 512], 2,
                                     reads=[r_sq[0], r_const], writes=rps(4, 2))
                            rms_from_psum(tmpa[:, :], r_ta, 4, 2, 128, 1.0 / KVL)
                            dve.op(lambda e: e.scalar_tensor_tensor(
                                out=ckvn[:, t0:t0 + 1024], in0=psb(0, 2), scalar=vcol(l, V_GKVL), in1=tmpa[:, :],
                                op0=ALU.mult, op1=ALU.mult), reads=rps(0, 2) + [r_ta, r_vecs], writes=[r_ckvn])
                            mm_group(lambda tb: psb(tb, 1, 96), 96, [wkr[:, k, :] for k in range(8)],
                                     lambda k, tb: hT[:, k, t0 + tb * 512:t0 + (tb + 1) * 512], 2,
                                     reads=[r_wsm] + r_h, writes=rps(0, 2))
                            mm_group(lambda tb: psb(2 + tb, 1, 96), 96, [wkrsw[:, k, :] for k in range(8)],
                                     lambda k, tb: hT[:, k, t0 + tb * 512:t0 + (tb + 1) * 512], 2,
                                     reads=[r_wsm] + r_h, writes=rps(2, 2))
                            act.op(lambda e: e.activation(out=sqh[0:96, 0, :], in_=psb(0, 2, 96), func=AF.Square),
                                   reads=rps(0, 2), writes=[r_sq[0]])
                            mm_group(lambda tb: psb(4 + tb, 1, 96), 96, [blk_b[0:96, 0:96]],
                                     lambda k, tb: sqh[0:96, 0, tb * 512:(tb + 1) * 512], 2,
                                     reads=[r_sq[0], r_const], writes=rps(4, 2))
                            rms_from_psum(tmpa[0:96, :], r_ta, 4, 2, 96, 1.0)
                            dve.op(lambda e: e.scalar_tensor_tensor(
                                out=tmpb[0:96, :], in0=psb(0, 2, 96), scalar=vcol(l, V_GK, 0, 96), in1=cosT[0:96, t0:t0 + 1024],
                                op0=ALU.mult, op1=ALU.mult), reads=rps(0, 2) + [r_tab, r_vecs], writes=[r_tb])
                            dve.op(lambda e: e.scalar_tensor_tensor(
                                out=tmpc[0:96, :], in0=psb(2, 2, 96), scalar=vcol(l, V_GKSW, 0, 96), in1=sinT[0:96, t0:t0 + 1024],
                                op0=ALU.mult, op1=ALU.mult), reads=rps(2, 2) + [r_tab, r_vecs], writes=[r_tc])
                            pool.op(lambda e: e.tensor_tensor(out=tmpb[0:96, :], in0=tmpb[0:96, :], in1=tmpc[0:96, :], op=ALU.add),
                                    reads=[r_tb, r_tc], writes=[r_tb])
                            pool.op(lambda e: e.tensor_tensor(out=kropeT[0:96, t0:t0 + 1024], in0=tmpb[0:96, :], in1=tmpa[0:96, :],
                                                              op=ALU.mult), reads=[r_tb, r_ta], writes=[r_krope])
                        wrel((l, "lat"))
                        if l == layer_ids[0]:
                            dump("cqn", cqn[:, :, :], [r_cqn], [128, 2, T])
                            dump("ckvn", ckvn[:, :], [r_ckvn], [128, T])
                            dump("kropeT", kropeT[0:96, :], [r_krope], [96, T])
                    barrier()

                    with ExitStack() as s2:
                        QT = sb("QT", [128, T], BF16, s2)
                        KT = sb("KT", [128, T], BF16, s2)
                        r_q, r_k = Reg("QT"), Reg("KT")
                        Vaug = sb("Vaug", [128, 16, 65], BF16, s2)
                        r_v = Reg("Vaug")
                        PT = [sb(f"PT{i}", [128, 1024], BF16, s2) for i in range(3)]
                        r_pt = [Reg(f"PT{i}") for i in range(3)]
                        ta = sb("ta2", [128, 1024], F32, s2)
                        tb_ = sb("tb2", [128, 1024], F32, s2)
                        rq = sb("rq2", [128, 1024], F32, s2)
                        sqa = sb("sqa", [128, 1024], BF16, s2)
                        r_ta, r_tb, r_rq, r_sqa = Reg("ta"), Reg("tb"), Reg("rq"), Reg("sqa")
                        rrow = sb("rrow", [128, 1024], F32, s2)
                        bcs = sb("bcs", [128, 1024], F32, s2)
                        r_rrow, r_bcs = Reg("rrow"), Reg("bcs")
                        yodd = sb("yodd", [128, T], BF16, s2)
                        r_yodd = Reg("yodd")
                        pool.op(lambda e: e.memset(Vaug[:, :, 64:65], 1.0), writes=[r_v])
                        for h in range(NH):
                            for hh in range(2):
                                t0 = hh * 1024
                                mm_group(lambda tb: psb(tb, 1, 96), 96, [wuq[:, k, h * 96:(h + 1) * 96] for k in range(2)],
                                         lambda k, tb: cqn[:, k, t0 + tb * 512:t0 + (tb + 1) * 512], 2,
                                         reads=[r_wsm, r_cqn], writes=rps(0, 2))
                                mm_group(lambda tb: psb(2 + tb, 1, 96), 96, [wuqsw[:, k, h * 96:(h + 1) * 96] for k in range(2)],
                                         lambda k, tb: cqn[:, k, t0 + tb * 512:t0 + (tb + 1) * 512], 2,
                                         reads=[r_wsm, r_cqn], writes=rps(2, 2))
                                act.op(lambda e: e.activation(out=sqa[0:96, :], in_=psb(0, 2, 96), func=AF.Square),
                                       reads=rps(0, 2), writes=[r_sqa])
                                mm_group(lambda tb: psb(4 + tb, 1, 96), 96, [blk_b[0:96, 0:96]],
                                         lambda k, tb: sqa[0:96, tb * 512:(tb + 1) * 512], 2,
                                         reads=[r_sqa, r_const], writes=rps(4, 2))
                                rms_from_psum(rq[0:96, :], r_rq, 4, 2, 96, 1.0)
                                dve.op(lambda e: e.scalar_tensor_tensor(
                                    out=ta[0:96, :], in0=psb(0, 2, 96), scalar=vcol(l, V_GQ, 0, 96), in1=cosT[0:96, t0:t0 + 1024],
                                    op0=ALU.mult, op1=ALU.mult), reads=rps(0, 2) + [r_tab, r_vecs], writes=[r_ta])
                                dve.op(lambda e: e.scalar_tensor_tensor(
                                    out=tb_[0:96, :], in0=psb(2, 2, 96), scalar=vcol(l, V_GQSW, 0, 96), in1=sinT[0:96, t0:t0 + 1024],
                                    op0=ALU.mult, op1=ALU.mult), reads=rps(2, 2) + [r_tab, r_vecs], writes=[r_tb])
                                pool.op(lambda e: e.tensor_tensor(out=ta[0:96, :], in0=ta[0:96, :], in1=tb_[0:96, :], op=ALU.add),
                                        reads=[r_ta, r_tb], writes=[r_ta])
                                pool.op(lambda e: e.tensor_tensor(out=QT[0:96, t0:t0 + 1024], in0=ta[0:96, :], in1=rq[0:96, :],
                                                                  op=ALU.mult), reads=[r_ta, r_rq], writes=[r_q])
                                mm_group(lambda tb: psb(6 + tb, 1, 64), 64, [wukv[:, h * 128:h * 128 + 64]],
                                         lambda k, tb: ckvn[:, t0 + tb * 512:t0 + (tb + 1) * 512], 2,
                                         reads=[r_wsm, r_ckvn], writes=rps(6, 2))
                                act.op(lambda e: e.activation(out=sqa[0:64, :], in_=psb(6, 2, 64), func=AF.Square),
                                       reads=rps(6, 2), writes=[r_sqa])
                                mm_group(lambda tb: psb(4 + tb, 1, 64), 64, [blk_b[0:64, 0:64]],
                                         lambda k, tb: sqa[0:64, tb * 512:(tb + 1) * 512], 2,
                                         reads=[r_sqa, r_const], writes=rps(4, 2))
                                rms_from_psum(rq[0:64, :], r_rq, 4, 2, 64, 1.0)
                                dve.op(lambda e: e.scalar_tensor_tensor(
                                    out=KT[0:64, t0:t0 + 1024], in0=psb(6, 2, 64), scalar=vcol(l, V_GK, 0, 64), in1=rq[0:64, :],
                                    op0=ALU.mult, op1=ALU.mult), reads=rps(6, 2) + [r_rq, r_vecs], writes=[r_k])
                            pool.op(lambda e: e.tensor_copy(out=KT[64:96, :], in_=kropeT[64:96, :]),
                                    reads=[r_krope], writes=[r_k])
                            def vfn(e, h=h):
                                last = None
                                for kc in range(16):
                                    last = e.matmul(ps[:, kc * 64:(kc + 1) * 64], lhsT=ckvn[:, kc * 128:(kc + 1) * 128],
                                                    rhs=wukv[:, h * 128 + 64:h * 128 + 128], start=True, stop=True, skip_group_check=True)
                                return last
                            pe.op(vfn, reads=[r_wsm, r_ckvn], writes=rps(0, 2))
                            act.op(lambda e: e.activation(out=Vaug[:, :, 0:64], in_=ps[:, 0:1024].rearrange("p (a b) -> p a b", b=64),
                                                          func=AF.Copy), reads=rps(0, 2), writes=[r_v])
                            if l == layer_ids[0] and h == 0:
                                dump("QT0", QT[0:96, :], [r_q], [96, T])
                                dump("KT0", KT[0:96, :], [r_k], [96, T])
                                dump("V0", Vaug[:, :, :], [r_v], [128, 16, 65])
                            ydst = yattn[0:64, h // 2, :] if h % 2 == 0 else yodd[0:64, :]
                            r_ydst = r_ya[h // 2] if h % 2 == 0 else r_yodd
                            for qh in range(2):
                                q0 = qh * 1024
                                def emit_s(kc):
                                    sbk = (kc % 2) * 2

                                    def sfn(e):
                                        for tb in range(2):
                                            e.matmul(psb(sbk + tb), lhsT=KT[0:96, kc * 128:(kc + 1) * 128],
                                                     rhs=QT[0:96, q0 + tb * 512:q0 + (tb + 1) * 512], start=True, stop=True,
                                                     skip_group_check=True)
                                    pe.op(sfn, reads=[r_k, r_q], writes=rps(sbk, 2))

                                emit_s(0)
                                for kc in range(16):
                                    sbk = (kc % 2) * 2
                                    pb = kc % 3
                                    if kc + 1 < 16:
                                        emit_s(kc + 1)
                                    act.op(lambda e: e.activation(out=PT[pb][:, :], in_=psb(sbk, 2), func=AF.Exp, scale=SCALE),
                                           reads=rps(sbk, 2), writes=[r_pt[pb]])

                                    def ofn(e):
                                        for tb in range(2):
                                            e.matmul(psb(4 + tb, 1, 65), lhsT=Vaug[:, kc, 0:65],
                                                     rhs=PT[pb][:, tb * 512:(tb + 1) * 512],
                                                     start=(kc == 0), stop=(kc == 15), skip_group_check=True)
                                    pe.op(ofn, reads=[r_v, r_pt[pb]], writes=rps(4, 2))
                                dve.op(lambda e: e.reciprocal(out=rrow[64:65, :], in_=psb(4, 2, 1, 64)),
                                       reads=rps(4, 2), writes=[r_rrow])
                                mm_group(lambda tb: psb(6 + tb, 1, 64), 64, [ones_f[64:65, 0:64]],
                                         lambda k, tb: rrow[64:65, tb * 512:(tb + 1) * 512], 2,
                                         reads=[r_rrow, r_const], writes=rps(6, 2))
                                act.op(lambda e: e.activation(out=bcs[0:64, :], in_=psb(6, 2, 64), func=AF.Copy),
                                       reads=rps(6, 2), writes=[r_bcs])
                                dve.op(lambda e, q0=q0, ydst=ydst: e.tensor_tensor(out=ydst[:, q0:q0 + 1024], in0=psb(4, 2, 64), in1=bcs[0:64, :],
                                                                                   op=ALU.mult),
                                       reads=rps(4, 2) + [r_bcs], writes=[r_ydst])
                            if h % 2 == 1:
                                sp.dma(yattn[64:128, h // 2, :], yodd[0:64, :], ds_yodd, reads=[r_yodd], writes=[r_ya[h // 2]])
                        if l == layer_ids[0]:
                            dump("yattn", yattn[:, :, :], r_ya, [128, 4, T])
                    barrier()
                barrier()

                with ExitStack() as s3:
                    hT = sb("hT", [128, 8, T], BF16, s3)
                    r_h = [Reg(f"h{c}") for c in range(8)]
                    yconv = sb("yconv", [128, 4, T], BF16, s3)
                    r_yc = [Reg(f"yc{c}") for c in range(4)]
                    make_hT(hT, r_h, l, V_GMIX)
                    with ExitStack() as s4:
                        tA = sb("tA", [128, T], F32, s4)
                        prod = sb("prod", [128, T + 2], F32, s4)
                        r_tA, r_prod = Reg("tA"), Reg("prod")
                        pool.op(lambda e: e.memset(prod[:, 0:1], 0.0), writes=[r_prod])
                        pool.op(lambda e: e.memset(prod[:, T + 1:T + 2], 0.0), writes=[r_prod])
                        wcc, r_wcc = wget((l, "cc"))
                        wcx, r_wcx = wget((l, "cx"))
                        wcb, r_wcb = wget((l, "cb"))
                        for j in range(4):
                            hrhs = lambda k, tb: hT[:, k, tb * 512:(tb + 1) * 512]
                            mm_group(lambda tb: psb(tb), 128, [wcc[:, k, j * 128:(j + 1) * 128] for k in range(8)], hrhs, 4,
                                     reads=[r_wcc] + r_h, writes=rps(0, 4))
                            act.op(lambda e: e.activation(out=tA[:, :], in_=psb(0, 4), func=AF.Copy), reads=rps(0, 4), writes=[r_tA])
                            mm_group(lambda tb: psb(4 + tb), 128, [wcx[:, k, j * 128:(j + 1) * 128] for k in range(8)], hrhs, 4,
                                     reads=[r_wcx] + r_h, writes=rps(4, 4))
                            dve.op(lambda e: e.tensor_tensor(out=prod[:, 1:T + 1], in0=psb(4, 4), in1=tA[:, :], op=ALU.mult),
                                   reads=rps(4, 4) + [r_tA], writes=[r_prod])
                            mm_group(lambda tb: psb(tb), 128, [wcb[:, k, j * 128:(j + 1) * 128] for k in range(8)], hrhs, 4,
                                     reads=[r_wcb] + r_h, writes=rps(0, 4))
                            act.op(lambda e, j=j: e.activation(out=tA[:, :], in_=prod[:, 1:T + 1], func=AF.Copy, scale=vcol(l, V_WC + 4 + j)),
                                   reads=[r_prod, r_vecs], writes=[r_tA])
                            dve.op(lambda e, j=j: e.scalar_tensor_tensor(out=tA[:, :], in0=prod[:, 0:T], scalar=vcol(l, V_WC + j), in1=tA[:, :],
                                                                          op0=ALU.mult, op1=ALU.add), reads=[r_prod, r_tA, r_vecs], writes=[r_tA])
                            dve.op(lambda e, j=j: e.scalar_tensor_tensor(out=tA[:, :], in0=prod[:, 2:T + 2], scalar=vcol(l, V_WC + 8 + j), in1=tA[:, :],
                                                                          op0=ALU.mult, op1=ALU.add), reads=[r_prod, r_tA, r_vecs], writes=[r_tA])
                            dve.op(lambda e, j=j: e.tensor_tensor(out=yconv[:, j, :], in0=psb(0, 4), in1=tA[:, :], op=ALU.mult),
                                   reads=rps(0, 4) + [r_tA], writes=[r_yc[j]])
                        wrel((l, "cc"))
                        wrel((l, "cx"))
                        wrel((l, "cb"))
                        if l == layer_ids[0]:
                            dump("yconv", yconv[:, :, :], r_yc, [128, 4, T])
                    barrier()
                    with ExitStack() as s5:
                        merged = sb("merged", [128, 8, 1024], BF16, s5)
                        r_mg = [Reg(f"mg{c}") for c in range(8)]
                        gaT = sb("gaT", [128, 1024], F32, s5)
                        gcT = sb("gcT", [128, 1024], F32, s5)
                        r_ga, r_gc = Reg("ga"), Reg("gc")
                        wcu = wsmall[:, 0:4096].rearrange("p (a b) -> p a b", b=1024)
                        pool.dma(wcu, w_cu_d[l].rearrange("(kc p) n -> p kc n", p=128), ds_wsm, writes=[r_wsm])
                        for hh in range(2):
                            t0 = hh * 1024
                            wau, r_wau = None, None
                            for j in range(8):
                                jg, jj = j // 4, j % 4
                                wga, r_wga = wget((l, "ga", hh, jg))
                                wgc, r_wgc = wget((l, "gc", hh, jg))
                                if wau is None:
                                    wau, r_wau = wget((l, "wau", hh))
                                hrhs = lambda k, tb: hT[:, k, t0 + tb * 512:t0 + (tb + 1) * 512]
                                mm_group(lambda tb: psb(tb), 128, [wga[:, k, jj * 128:(jj + 1) * 128] for k in range(8)], hrhs, 2,
                                         reads=[r_wga] + r_h, writes=rps(0, 2))
                                act.op(lambda e, j=j: e.activation(out=gaT[:, :], in_=psb(0, 2), func=AF.Sigmoid, bias=vcol(l, V_BG + j)),
                                       reads=rps(0, 2) + [r_vecs], writes=[r_ga])
                                mm_group(lambda tb: psb(2 + tb), 128, [wau[:, k, j * 128:(j + 1) * 128] for k in range(4)],
                                         lambda k, tb: yattn[:, k, t0 + tb * 512:t0 + (tb + 1) * 512], 2,
                                         reads=[r_wau] + r_ya, writes=rps(2, 2))
                                dve.op(lambda e: e.tensor_tensor(out=gaT[:, :], in0=psb(2, 2), in1=gaT[:, :], op=ALU.mult),
                                       reads=rps(2, 2) + [r_ga], writes=[r_ga])
                                mm_group(lambda tb: psb(4 + tb), 128, [wgc[:, k, jj * 128:(jj + 1) * 128] for k in range(8)], hrhs, 2,
                                         reads=[r_wgc] + r_h, writes=rps(4, 2))
                                act.op(lambda e, j=j: e.activation(out=gcT[:, :], in_=psb(4, 2), func=AF.Sigmoid, bias=vcol(l, V_BG + 8 + j)),
                                       reads=rps(4, 2) + [r_vecs], writes=[r_gc])
                                mm_group(lambda tb: psb(6 + tb), 128, [wcu[:, k, j * 128:(j + 1) * 128] for k in range(4)],
                                         lambda k, tb: yconv[:, k, t0 + tb * 512:t0 + (tb + 1) * 512], 2,
                                         reads=[r_wsm] + r_yc, writes=rps(6, 2))
                                dve.op(lambda e: e.tensor_tensor(out=gcT[:, :], in0=psb(6, 2), in1=gcT[:, :], op=ALU.mult),
                                       reads=rps(6, 2) + [r_gc], writes=[r_gc])
                                pool.op(lambda e, j=j: e.tensor_tensor(out=merged[:, j, :], in0=gaT[:, :], in1=gcT[:, :], op=ALU.add),
                                        reads=[r_ga, r_gc], writes=[r_mg[j]])
                                if jj == 3:
                                    wrel((l, "ga", hh, jg))
                                    wrel((l, "gc", hh, jg))
                            wrel((l, "wau", hh))
                            for m in range(8):
                                mg, mm = m // 4, m % 4
                                wo, r_wo = wget((l, "wo", hh, mg))
                                b0 = (m % 4) * 2
                                mm_group(lambda tb, b0=b0: psb(b0 + tb), 128, [wo[:, k, mm * 128:(mm + 1) * 128] for k in range(8)],
                                         lambda k, tb: merged[:, k, tb * 512:(tb + 1) * 512], 2,
                                         reads=[r_wo] + r_mg, writes=rps(b0, 2))
                                dve.op(lambda e, m=m, b0=b0: e.tensor_tensor(out=xT[:, m, t0:t0 + 1024], in0=psb(b0, 2), in1=xT[:, m, t0:t0 + 1024],
                                                                              op=ALU.add), reads=rps(b0, 2) + [r_x[m]], writes=[r_x[m]])
                                if mm == 3:
                                    wrel((l, "wo", hh, mg))
                    barrier()
                barrier()
            barrier()
            if l == layer_ids[0]:
                dump("x_mix", xT[:, :, :], r_x, [128, 8, T])

            with ExitStack() as s6:
                hT = sb("hT", [128, 8, T], BF16, s6)
                r_h = [Reg(f"h{c}") for c in range(8)]
                actT = sb("actT", [128, FFN_MAXCH, T], BF16, s6)
                r_actc = [Reg(f"act{c}") for c in range(FFN_MAXCH)]
                u0 = sb("u0", [128, T], F32, s6)
                u1 = sb("u1", [128, T], F32, s6)
                sT = sb("sT", [128, T], F32, s6)
                r_u = [Reg("u0"), Reg("u1")]
                r_s = Reg("sT")
                norm_stats([actT[:, 0, :], actT[:, 1, :]], [r_actc[0], r_actc[1]], ones_b[:, :])
                make_hT(hT, r_h, l, V_GFFN)
                hrhs = lambda k, tb: hT[:, k, tb * 512:(tb + 1) * 512]
                for g, (p_lo, p_hi) in enumerate(FFN_GROUPS):
                    chunks = []
                    for pp in range(p_lo, p_hi):
                        wa, r_wa = wget((l, "ua", pp))
                        wv, r_wv = wget((l, "uv", pp))
                        npair = 4 if pp < 5 else 2
                        for jj in range(npair):
                            j = pp * 4 + jj
                            ci = len(chunks)
                            chunks.append(j)
                            for which, (wp, r_wp, half, ccol) in enumerate(((wa, r_wa, 0, j), (wv, r_wv, 1, 22 + j))):
                                b0 = half * 4
                                u = (u0, u1)[which]
                                r_uu = r_u[which]
                                mm_group(lambda tb, b0=b0: psb(b0 + tb), 128, [wp[:, k, jj * 128:(jj + 1) * 128] for k in range(8)], hrhs, 4,
                                         reads=[r_wp] + r_h, writes=rps(b0, 4))
                                act.op(lambda e, u=u, b0=b0, ccol=ccol: e.activation(out=u[:, :], in_=psb(b0, 4), func=AF.Copy,
                                                                                    scale=vcol(l, V_WFC + 44 + ccol)),
                                       reads=rps(b0, 4) + [r_vecs], writes=[r_uu])
                                dve.op(lambda e, u=u, b0=b0, ccol=ccol: e.scalar_tensor_tensor(
                                    out=u[:, 1:T], in0=ps[:, b0 * 512:b0 * 512 + T - 1], scalar=vcol(l, V_WFC + ccol), in1=u[:, 1:T],
                                    op0=ALU.mult, op1=ALU.add), reads=rps(b0, 4) + [r_uu, r_vecs], writes=[r_uu])
                                dve.op(lambda e, u=u, b0=b0, ccol=ccol: e.scalar_tensor_tensor(
                                    out=u[:, 0:T - 1], in0=ps[:, b0 * 512 + 1:b0 * 512 + T], scalar=vcol(l, V_WFC + 88 + ccol), in1=u[:, 0:T - 1],
                                    op0=ALU.mult, op1=ALU.add), reads=rps(b0, 4) + [r_uu, r_vecs], writes=[r_uu])
                                if which == 0:
                                    act.op(lambda e: e.activation(out=sT[:, :], in_=u0[:, :], func=AF.Silu), reads=[r_u[0]], writes=[r_s])
                                else:
                                    pool.op(lambda e, ci=ci: e.tensor_tensor(out=actT[:, ci, :], in0=sT[:, :], in1=u1[:, :], op=ALU.mult),
                                            reads=[r_s, r_u[1]], writes=[r_actc[ci]])
                        wrel((l, "ua", pp))
                        wrel((l, "uv", pp))
                    if l == layer_ids[0] and g == 0:
                        dump("act0", actT[:, 0, :], [r_actc[0]], [128, T])
                    wds = []
                    for pp in range(p_lo, p_hi):
                        wds.append(wget((l, "dn", pp)))
                    nch = len(chunks)
                    for m in range(8):
                        b0 = (m % 2) * 4
                        lhs = [wds[ci // 4][0][:, ci % 4, m * 128:(m + 1) * 128] for ci in range(nch)]
                        mm_group(lambda tb, b0=b0: psb(b0 + tb), 128, lhs,
                                 lambda k, tb: actT[:, k, tb * 512:(tb + 1) * 512], 4,
                                 reads=[w[1] for w in wds] + r_actc[0:nch], writes=rps(b0, 4))
                        dve.op(lambda e, m=m, b0=b0: e.tensor_tensor(out=xT[:, m, :], in0=psb(b0, 4), in1=xT[:, m, :], op=ALU.add),
                               reads=rps(b0, 4) + [r_x[m]], writes=[r_x[m]])
                    for pp in range(p_lo, p_hi):
                        wrel((l, "dn", pp))
            barrier()
            if l == layer_ids[0]:
                dump("x_ffn", xT[:, :, :], r_x, [128, 8, T])

            with ExitStack() as s7:
                hT = sb("hT", [128, 8, T], BF16, s7)
                r_h = [Reg(f"h{c}") for c in range(8)]
                pTb = sb("pTb", [128, 2, T], BF16, s7)
                r_p = Reg("pT")
                pg0 = sb("pg0", [128, T], F32, s7)
                pg1 = sb("pg1", [128, T], F32, s7)
                r_pg = [Reg("pg0"), Reg("pg1")]
                sq0 = sb("sq0b", [128, T], BF16, s7)
                sq1 = sb("sq1b", [128, T], BF16, s7)
                pool.dma(pTb[:, :, :], pT_d[l].rearrange("(kc p) t -> p kc t", p=128), ds_p, writes=[r_p])
                norm_stats([sq0, sq1], [Reg("sq0"), Reg("sq1")], ones_b[:, :])
                make_hT(hT, r_h, l, V_GPLE)
                hrhs = lambda k, tb: hT[:, k, tb * 512:(tb + 1) * 512]
                wpl, r_wpl = wget((l, "pl"))
                for m in range(8):
                    mg, mm = m // 4, m % 4
                    wpg, r_wpg = wget((l, "pg", mg))
                    pg = (pg0, pg1)[m % 2]
                    r_pgm = r_pg[m % 2]
                    mm_group(lambda tb: psb(tb), 128, [wpg[:, k, mm * 128:(mm + 1) * 128] for k in range(8)], hrhs, 4,
                             reads=[r_wpg] + r_h, writes=rps(0, 4))
                    act.op(lambda e, pg=pg: e.activation(out=pg[:, :], in_=psb(0, 4), func=AF.Sigmoid), reads=rps(0, 4), writes=[r_pgm])
                    mm_group(lambda tb: psb(4 + tb), 128, [wpl[:, k, m * 128:(m + 1) * 128] for k in range(2)],
                             lambda k, tb: pTb[:, k, tb * 512:(tb + 1) * 512], 4,
                             reads=[r_wpl, r_p], writes=rps(4, 4))
                    dve.op(lambda e, pg=pg: e.tensor_tensor(out=pg[:, :], in0=psb(4, 4), in1=pg[:, :], op=ALU.mult),
                           reads=rps(4, 4) + [r_pgm], writes=[r_pgm])
                    pool.op(lambda e, pg=pg, m=m: e.tensor_tensor(out=xT[:, m, :], in0=xT[:, m, :], in1=pg[:, :], op=ALU.add),
                            reads=[r_pgm, r_x[m]], writes=[r_x[m]])
                    if mm == 3:
                        wrel((l, "pg", mg))
                wrel((l, "pl"))
            barrier()

        for c in range(8):
            sp.dma(outT_d[c * 128:(c + 1) * 128, :], xT[:, c, :], ds_out, reads=[r_x[c]])
        sp.wait(Ev(ds_out[0], ds_out[1]))
        if dbg and ds_dbg[1] > 0:
            sp.wait(Ev(ds_dbg[0], ds_dbg[1]))
        for e in engines:
            if e.seq > 0 and e is not sp:
                sp.wait(Ev(e.sem, e.seq))
        for ds in all_ds:
            if ds[1] > 0:
                sp.wait(Ev(ds[0], ds[1]))

        with nc.Block() as block:
            @block.tensor
            def _(e):
                for t in pe.thunks:
                    t(e)

            @block.scalar
            def _(e):
                for t in act.thunks:
                    t(e)

            @block.vector
            def _(e):
                for t in dve.thunks:
                    t(e)

            @block.gpsimd
            def _(e):
                for t in pool.thunks:
                    t(e)

            @block.sync
            def _(e):
                for t in sp.thunks:
                    t(e)
    return dbg_out


FFN_GROUPS = [(0, 2), (2, 4), (4, 6)]
FFN_MAXCH = 8


def _pack_vecs(inp, l):
    v = np.zeros((128, NV), np.float32)
    v[:, V_GMIX:V_GMIX + 8] = inp["g_mix"][l].reshape(8, 128).T
    v[:, V_GFFN:V_GFFN + 8] = inp["g_ffn"][l].reshape(8, 128).T
    v[:, V_GPLE:V_GPLE + 8] = inp["g_ple"][l].reshape(8, 128).T
    v[:, V_GQL:V_GQL + 2] = inp["g_q_lat"][l].reshape(2, 128).T
    v[:, V_GKVL] = inp["g_kv_lat"][l]
    v[:, V_BG:V_BG + 16] = inp["b_gate"][l].reshape(16, 128).T
    wc = inp["w_conv"][l]
    for k in range(3):
        v[:, V_WC + k * 4:V_WC + k * 4 + 4] = wc[k].reshape(4, 128).T
    wf = inp["w_ffn_conv"][l]
    for k in range(3):
        v[:, V_WFC + k * 44:V_WFC + k * 44 + 44] = wf[k].reshape(44, 128).T
    sw = np.concatenate([np.arange(64), 80 + np.arange(16), 64 + np.arange(16)])
    gq = inp["g_q_head"][l]
    gk = inp["g_k_head"][l]
    v[0:96, V_GQ] = gq
    v[0:96, V_GQSW] = gq[sw]
    v[0:96, V_GK] = gk
    v[0:96, V_GKSW] = gk[sw]
    inv = (10000.0 ** (-np.arange(0, 32, 2, dtype=np.float32) / np.float32(32))).astype(np.float32)
    v[64:96, V_INV] = np.concatenate([inv, inv])
    v[64:80, V_SGN] = -1.0
    v[80:96, V_SGN] = 1.0
    return v


def _prep_shared(inp):
    f = lambda a: np.ascontiguousarray(np.asarray(a, dtype=np.float32))
    w_in = f(inp["w_in"])
    L = w_in.shape[0]
    w_kr = np.zeros((L, D, 96), np.float32)
    w_kr[:, :, 64:96] = w_in[:, :, 384:416]
    w_kr_sw = np.zeros((L, D, 96), np.float32)
    w_kr_sw[:, :, 64:80] = w_in[:, :, 400:416]
    w_kr_sw[:, :, 80:96] = w_in[:, :, 384:400]
    w_uq = f(inp["w_uq"])
    idx = np.arange(NH * DQK).reshape(NH, DQK).copy()
    sw = np.concatenate([np.arange(64), 80 + np.arange(16), 64 + np.arange(16)])
    idx = idx[:, sw].reshape(-1)
    w_uq_sw = np.ascontiguousarray(w_uq[:, :, idx])
    vecs = np.concatenate([_pack_vecs(inp, l) for l in range(L)], axis=1)
    return {
        "vecs": np.ascontiguousarray(vecs),
        "w_in": w_in, "w_kr": w_kr, "w_kr_sw": w_kr_sw, "w_uq": w_uq, "w_uq_sw": w_uq_sw,
        "w_ukv": f(inp["w_ukv"]), "w_attn_up": f(inp["w_attn_up"]), "w_conv_up": f(inp["w_conv_up"]),
        "w_o": f(inp["w_o"]), "w_up": f(inp["w_up"]), "w_down": f(inp["w_down"]),
        "w_ple_gate": f(inp["w_ple_gate"]), "w_ple": f(inp["w_ple"]),
    }


def _core_inputs(inp, shared, b):
    x = np.asarray(inp["x"], dtype=np.float32)
    p = np.asarray(inp["p"], dtype=np.float32)
    pos = np.asarray(inp["positions"]).astype(np.int32)
    m = dict(shared)
    m["xT"] = np.ascontiguousarray(x[b].T)
    m["pT"] = np.ascontiguousarray(np.transpose(p[:, b], (0, 2, 1)))
    m["pos"] = np.ascontiguousarray(pos[b][None, :])
    return m


def kernel(**inputs):
    shared = _prep_shared(inputs)
    in_maps = [_core_inputs(inputs, shared, b) for b in range(NCORES)]
    nc = bass.Bass("TRN2", target_bir_lowering=False)
    build_program(nc, list(range(DEPTH)))
    res = run_bass_kernel_spmd(nc, in_maps, core_ids=list(range(NCORES)))
    out = np.stack([np.ascontiguousarray(res.results[b]["outT"].T) for b in range(NCORES)], axis=0)
    return out.astype(np.float32)
```
